# Optimizing a Trainium2 kernel written in Bass

```python
import math
import jax, jax.numpy as jnp
from jax import lax
import numpy as np

D_MODEL = 4096
BATCH = 2
SEQ = 4096
DEPTH = 2

MEM_LEN = 256
N_BRANCH = 4
BRANCH_WIDTH = D_MODEL // 4
POOL_WINDOWS = (2, 4, 8, 16)
POOL_GROUP = BRANCH_WIDTH // 4
GLA_HEADS = 4
GLA_DK = BRANCH_WIDTH // 2
GLA_DV = BRANCH_WIDTH
GLA_HEAD_DK = GLA_DK // GLA_HEADS
GLA_HEAD_DV = GLA_DV // GLA_HEADS
GLA_RANK = 16
GLA_TAU = 16.0
GLA_CHUNK = 64
SWA_HEAD_DIM = 64
SWA_Q_HEADS = BRANCH_WIDTH // SWA_HEAD_DIM
SWA_KV_HEADS = SWA_Q_HEADS // 8
SWA_GROUP = SWA_Q_HEADS // SWA_KV_HEADS
SWA_KV_WIDTH = SWA_KV_HEADS * SWA_HEAD_DIM
WINDOW = 128
SWA_BLOCK = 128
ROPE_THETA = 500000.0
ROPE_DIM = SWA_HEAD_DIM // 4
XA_HEADS = 4
XA_HEAD_DIM = BRANCH_WIDTH // XA_HEADS
LN_EPS = 1e-5
DEEPNORM_ALPHA = (2 * DEPTH) ** 0.25
DEEPNORM_BETA = (8 * DEPTH) ** -0.25

IN_SPLITS = (
    BRANCH_WIDTH, BRANCH_WIDTH,
    GLA_DK, GLA_DK, GLA_DV, GLA_DV, GLA_RANK,
    BRANCH_WIDTH, SWA_KV_WIDTH, SWA_KV_WIDTH, BRANCH_WIDTH,
    BRANCH_WIDTH, BRANCH_WIDTH,
    N_BRANCH * D_MODEL,
)
D_IN = sum(IN_SPLITS)

kernel_name = "hybrid_pool_gla_swa_mem_deepnorm"


def _split_points():
    return [int(v) for v in np.cumsum(IN_SPLITS)[:-1]]


def layer_norm(x, g, b):
    xf = x.astype(jnp.float32)
    mu = jnp.mean(xf, axis=-1, keepdims=True)
    var = jnp.mean(jnp.square(xf - mu), axis=-1, keepdims=True)
    y = (xf - mu) * lax.rsqrt(var + LN_EPS) * g.astype(jnp.float32) + b.astype(jnp.float32)
    return y.astype(x.dtype)


def rope_partial(x, positions):
    half = ROPE_DIM // 2
    inv_freq = ROPE_THETA ** (-jnp.arange(0, ROPE_DIM, 2, dtype=jnp.float32) / ROPE_DIM)
    ang = positions.astype(jnp.float32)[:, :, None] * inv_freq
    cos = jnp.cos(ang)[:, :, None, :]
    sin = jnp.sin(ang)[:, :, None, :]
    xf = x.astype(jnp.float32)
    x1, x2, rest = xf[..., :half], xf[..., half:ROPE_DIM], xf[..., ROPE_DIM:]
    out = jnp.concatenate([x1 * cos - x2 * sin, x2 * cos + x1 * sin, rest], axis=-1)
    return out.astype(x.dtype)


def pool_mixer(u, w_pool, pool_scale):
    S = u.shape[1]
    t = jnp.arange(S)
    outs = []
    for gi, w in enumerate(POOL_WINDOWS):
        ug = u[..., gi * POOL_GROUP:(gi + 1) * POOL_GROUP].astype(jnp.float32)
        c = jnp.cumsum(ug, axis=1)
        c_lag = jnp.pad(c, ((0, 0), (w, 0), (0, 0)))[:, :S]
        cnt = jnp.minimum(t + 1, w).astype(jnp.float32)[None, :, None]
        pooled = (c - c_lag) / cnt - ug
        outs.append(jnp.einsum('bsc,cd->bsd', pooled.astype(u.dtype), w_pool[gi]))
    return jnp.concatenate(outs, axis=-1) * pool_scale


def gla_mixer(q, k, v, lr, w_gla_up, b_gla, gla_norm):
    B, S, _ = q.shape
    C = GLA_CHUNK
    n = S // C
    log_a = jax.nn.log_sigmoid((lr @ w_gla_up + b_gla).astype(jnp.float32)) / GLA_TAU

    def to_chunks(a, hd):
        return a.astype(jnp.float32).reshape(B, n, C, GLA_HEADS, hd).transpose(1, 0, 3, 2, 4)

    qc = to_chunks(q, GLA_HEAD_DK) * (GLA_HEAD_DK ** -0.5)
    kc = to_chunks(k, GLA_HEAD_DK)
    vc = to_chunks(v, GLA_HEAD_DV)
    gc = to_chunks(log_a, GLA_HEAD_DK)
    tri = jnp.tril(jnp.ones((C, C), dtype=bool))

    def step(state, inp):
        qb, kb, vb, gb = inp
        b = jnp.cumsum(gb, axis=2)
        o_inter = jnp.einsum('bhcd,bhde->bhce', qb * jnp.exp(b), state)
        diff = b[:, :, :, None, :] - b[:, :, None, :, :]
        diff = jnp.where(tri[None, None, :, :, None], diff, -jnp.inf)
        attn = jnp.sum(qb[:, :, :, None, :] * kb[:, :, None, :, :] * jnp.exp(diff), axis=-1)
        o_intra = jnp.einsum('bhij,bhje->bhie', attn, vb)
        b_last = b[:, :, -1:, :]
        k_dec = kb * jnp.exp(b_last - b)
        state = jnp.exp(b_last[:, :, 0, :])[..., None] * state + jnp.einsum('bhcd,bhce->bhde', k_dec, vb)
        return state, o_inter + o_intra

    state0 = jnp.zeros((B, GLA_HEADS, GLA_HEAD_DK, GLA_HEAD_DV), jnp.float32)
    _, o = lax.scan(step, state0, (qc, kc, vc, gc))
    o = o.transpose(1, 0, 3, 2, 4).reshape(B, S, GLA_HEADS, GLA_HEAD_DV)
    o = o * lax.rsqrt(jnp.mean(jnp.square(o), axis=-1, keepdims=True) + LN_EPS)
    o = o.reshape(B, S, GLA_DV) * gla_norm.astype(jnp.float32)
    return o.astype(q.dtype)


def swa_mixer(q, k, v, positions, sinks):
    B, S, _ = q.shape
    nb = S // SWA_BLOCK
    q = rope_partial(q.reshape(B, S, SWA_Q_HEADS, SWA_HEAD_DIM), positions) * (SWA_HEAD_DIM ** -0.5)
    k = rope_partial(k.reshape(B, S, SWA_KV_HEADS, SWA_HEAD_DIM), positions)
    v = v.reshape(B, S, SWA_KV_HEADS, SWA_HEAD_DIM)
    qb = q.reshape(B, nb, SWA_BLOCK, SWA_KV_HEADS, SWA_GROUP, SWA_HEAD_DIM)

    def band(a):
        ap = jnp.pad(a, ((0, 0), (SWA_BLOCK, 0), (0, 0), (0, 0)))
        ap = ap.reshape(B, nb + 1, SWA_BLOCK, SWA_KV_HEADS, SWA_HEAD_DIM)
        return jnp.concatenate([ap[:, :-1], ap[:, 1:]], axis=2)

    kb, vb = band(k), band(v)
    s = jnp.einsum('bnqkgd,bnskd->bnkgqs', qb, kb).astype(jnp.float32)
    qi = jnp.arange(SWA_BLOCK)
    sj = jnp.arange(2 * SWA_BLOCK)
    blk = jnp.arange(nb)
    diff = qi[:, None] + SWA_BLOCK - sj[None, :]
    key_abs = blk[:, None] * SWA_BLOCK - SWA_BLOCK + sj[None, :]
    valid = ((diff >= 0) & (diff < WINDOW))[None, :, :] & (key_abs >= 0)[:, None, :]
    s = jnp.where(valid[None, :, None, None, :, :], s, -jnp.inf)
    sink = sinks.astype(jnp.float32).reshape(SWA_KV_HEADS, SWA_GROUP)[None, None, :, :, None, None]
    sink = jnp.broadcast_to(sink, s.shape[:-1] + (1,))
    p = jax.nn.softmax(jnp.concatenate([s, sink], axis=-1), axis=-1)[..., :-1]
    o = jnp.einsum('bnkgqs,bnskd->bnqkgd', p.astype(vb.dtype), vb)
    return o.reshape(B, S, BRANCH_WIDTH)


def mem_attn(q, mem, w_mem_kv):
    B, S, _ = q.shape
    M = mem.shape[1]
    mkv = mem @ w_mem_kv
    mk = mkv[..., :BRANCH_WIDTH].reshape(B, M, XA_HEADS, XA_HEAD_DIM)
    mv = mkv[..., BRANCH_WIDTH:].reshape(B, M, XA_HEADS, XA_HEAD_DIM)
    qh = q.reshape(B, S, XA_HEADS, XA_HEAD_DIM)
    s = jnp.einsum('bshd,bmhd->bhsm', qh, mk).astype(jnp.float32) * (XA_HEAD_DIM ** -0.5)
    p = jax.nn.softmax(s, axis=-1)
    o = jnp.einsum('bhsm,bmhd->bshd', p.astype(mv.dtype), mv)
    return o.reshape(B, S, BRANCH_WIDTH)


def hybrid_layer(x, mem, positions, w_in, w_pool, pool_scale, w_gla_up, b_gla, gla_norm,
                 sinks, w_mem_kv, w_branch, w_out, ln_g, ln_b):
    h = x @ w_in
    (pool_u, pool_g, gq, gk, gv, gg, glr, sq, sk, sv, sg, xq, xg, mg) = jnp.split(h, _split_points(), axis=-1)
    o_pool = pool_mixer(pool_u, w_pool, pool_scale) * jax.nn.silu(pool_g)
    o_gla = gla_mixer(gq, gk, gv, glr, w_gla_up, b_gla, gla_norm) * jax.nn.silu(gg)
    o_swa = swa_mixer(sq, sk, sv, positions, sinks) * jax.nn.silu(sg)
    o_mem = mem_attn(xq, mem, w_mem_kv) * jax.nn.silu(xg)
    branches = (o_pool, o_gla, o_swa, o_mem)
    y = None
    for bi in range(N_BRANCH):
        gate = jax.nn.sigmoid(mg[..., bi * D_MODEL:(bi + 1) * D_MODEL])
        term = gate * (branches[bi] @ w_branch[bi])
        y = term if y is None else y + term
    out = y @ w_out
    return layer_norm(DEEPNORM_ALPHA * x + out, ln_g, ln_b)


def setup_inputs(seed: int = 0) -> dict:
    key = jax.random.key(seed)
    ks = jax.random.split(key, 16)
    f32 = jnp.float32
    x = jax.random.normal(ks[0], (BATCH, SEQ, D_MODEL), f32)
    mem = jax.random.normal(ks[1], (BATCH, MEM_LEN, D_MODEL), f32)
    offset = jax.random.randint(ks[2], (BATCH, 1), 0, 1024, dtype=jnp.int32)
    positions = (offset + jnp.arange(SEQ, dtype=jnp.int32)[None, :]).astype(jnp.int32)
    w_in = jax.random.normal(ks[3], (DEPTH, D_MODEL, D_IN), f32) * D_MODEL ** -0.5
    w_pool = jax.random.normal(ks[4], (DEPTH, len(POOL_WINDOWS), POOL_GROUP, POOL_GROUP), f32) * POOL_GROUP ** -0.5
    pool_scale = 1.0 + 0.02 * jax.random.normal(ks[5], (DEPTH, BRANCH_WIDTH), f32)
    w_gla_up = jax.random.normal(ks[6], (DEPTH, GLA_RANK, GLA_DK), f32) * GLA_RANK ** -0.5
    b_gla = 0.01 * jax.random.normal(ks[7], (DEPTH, GLA_DK), f32)
    gla_norm = 1.0 + 0.02 * jax.random.normal(ks[8], (DEPTH, GLA_DV), f32)
    sinks = 0.5 * jax.random.normal(ks[9], (DEPTH, SWA_Q_HEADS), f32)
    w_mem_kv = jax.random.normal(ks[10], (DEPTH, D_MODEL, 2 * BRANCH_WIDTH), f32) * D_MODEL ** -0.5
    w_branch = jax.random.normal(ks[11], (DEPTH, N_BRANCH, BRANCH_WIDTH, D_MODEL), f32) * (BRANCH_WIDTH ** -0.5 * DEEPNORM_BETA)
    w_out = jax.random.normal(ks[12], (DEPTH, D_MODEL, D_MODEL), f32) * (D_MODEL ** -0.5 * DEEPNORM_BETA)
    ln_g = 1.0 + 0.02 * jax.random.normal(ks[13], (DEPTH, D_MODEL), f32)
    ln_b = 0.02 * jax.random.normal(ks[14], (DEPTH, D_MODEL), f32)
    return {"x": x, "mem": mem, "positions": positions, "w_in": w_in, "w_pool": w_pool,
            "pool_scale": pool_scale, "w_gla_up": w_gla_up, "b_gla": b_gla, "gla_norm": gla_norm,
            "sinks": sinks, "w_mem_kv": w_mem_kv, "w_branch": w_branch, "w_out": w_out,
            "ln_g": ln_g, "ln_b": ln_b}


def reference(x, mem, positions, w_in, w_pool, pool_scale, w_gla_up, b_gla, gla_norm,
              sinks, w_mem_kv, w_branch, w_out, ln_g, ln_b):
    for l in range(DEPTH):
        x = hybrid_layer(x, mem, positions, w_in[l], w_pool[l], pool_scale[l], w_gla_up[l],
                         b_gla[l], gla_norm[l], sinks[l], w_mem_kv[l], w_branch[l], w_out[l],
                         ln_g[l], ln_b[l])
    return x
```

```python
import numpy as np
import concourse.bass as bass
import concourse.mybir as mybir
from contextlib import ExitStack

F32 = mybir.dt.float32
BF16 = mybir.dt.bfloat16
I32 = mybir.dt.int32
AF = mybir.ActivationFunctionType
ALU = mybir.AluOpType
AX = mybir.AxisListType
DTSIZE = {F32: 4, BF16: 2, I32: 4}


class Op:
    __slots__ = ("eng", "fn", "deps", "inc", "val", "isdma", "sem", "idx", "tag")


def _box(ap):
    name = ap.tensor.name
    sz = DTSIZE[ap.dtype]
    pairs = ap.ap
    off = ap.offset
    sp = str(ap.space)
    if "DRAM" in sp.upper() or "HBM" in sp.upper():
        span = sum((c - 1) * abs(s) for s, c in pairs)
        return (name, 0, 1, off * sz, (off + span + 1) * sz)
    pstep, pcnt = pairs[0]
    if pstep == 0:
        pstep = 1 << 40
    p0 = off // pstep
    f0 = off % pstep
    span = sum((c - 1) * abs(s) for s, c in pairs[1:])
    return (name, p0, p0 + pcnt, f0 * sz, (f0 + span + 1) * sz)


def _ovl(a, b):
    return a[1] < b[2] and b[1] < a[2] and a[3] < b[4] and b[3] < a[4]


def _cov(a, b):
    return a[1] <= b[1] and a[2] >= b[2] and a[3] <= b[3] and a[4] >= b[4]


class Prog:
    ENGS = ("pe", "act", "dve", "pool", "sp")
    NDS = 40
    SAME_ENG_SYNC = True

    def __init__(self, nc, stack):
        self.nc = nc
        self.ops = {e: [] for e in self.ENGS}
        self.acc = {}
        self.esem = {e: stack.enter_context(nc.semaphore("s_" + e)) for e in ("pe", "act", "dve", "pool")}
        self.dsem = [stack.enter_context(nc.semaphore("d%d" % i)) for i in range(self.NDS)]
        self.dlast = [None] * self.NDS
        self.dval = [0] * self.NDS
        self.dnext = 0
        self.alldma = []

    def add(self, eng, fn, reads=(), writes=(), dma=0, tag=""):
        op = Op()
        op.eng = eng
        op.fn = fn
        op.inc = False
        op.val = None
        op.isdma = dma
        op.sem = None
        op.tag = tag
        deps = set()
        rb = [_box(a) for a in reads]
        wb = [_box(a) for a in writes]
        for b in rb:
            for a in self.acc.get(b[0], ()):
                if a[1] and _ovl(a[2], b):
                    deps.add(a[0])
        for b in wb:
            for a in self.acc.get(b[0], ()):
                if _ovl(a[2], b):
                    deps.add(a[0])
        for b in wb:
            lst = self.acc.setdefault(b[0], [])
            lst[:] = [a for a in lst if not _cov(b, a[2])]
            lst.append((op, True, b))
        for b in rb:
            lst = self.acc.setdefault(b[0], [])
            if not dma:
                lst[:] = [a for a in lst if not (a[0].eng == eng and not a[0].isdma and not a[1] and a[2] == b)]
            lst.append((op, False, b))
        deps.discard(op)
        if dma:
            i = self.dnext
            self.dnext = (self.dnext + 1) % self.NDS
            if self.dlast[i] is not None:
                deps.add(self.dlast[i])
            self.dlast[i] = op
            self.dval[i] += 16 * dma
            op.sem = i
            op.val = self.dval[i]
            self.alldma.append(op)
        fdeps = []
        for d in deps:
            if d.isdma:
                fdeps.append(d)
            elif d.eng == eng:
                if eng != "pe" and self.SAME_ENG_SYNC:
                    fdeps.append(d)
                    d.inc = True
            else:
                fdeps.append(d)
                d.inc = True
        op.deps = fdeps
        op.idx = len(self.ops[eng])
        self.ops[eng].append(op)
        return op

    def dma(self, eng, out, in_, **kw):
        def fn(e):
            return [e.dma_start(out=out, in_=in_, **kw)]
        return self.add(eng, fn, reads=[in_], writes=[out], dma=1)

    def dmas(self, eng, pairs):
        def fn(e):
            return [e.dma_start(out=o, in_=i) for o, i in pairs]
        return self.add(eng, fn, reads=[i for o, i in pairs], writes=[o for o, i in pairs], dma=len(pairs))

    def mm(self, out, pairs, extra_reads=(), **kw):
        n = len(pairs)

        def fn(e):
            ins = None
            for k, (l, r) in enumerate(pairs):
                ins = e.matmul(out, l, r, start=(k == 0), stop=(k == n - 1), **kw)
            return ins
        reads = []
        for l, r in pairs:
            reads.append(l)
            reads.append(r)
        return self.add("pe", fn, reads=reads, writes=[out])

    def transpose(self, out, in_, ident):
        def fn(e):
            return e.transpose(out, in_, ident)
        return self.add("pe", fn, reads=[in_, ident], writes=[out])

    def act(self, out, in_, func, bias=None, scale=None, accum_out=None, eng="act"):
        kw = {}
        reads = [in_]
        if bias is not None:
            kw["bias"] = bias
            if not isinstance(bias, (int, float)):
                reads.append(bias)
        if scale is not None:
            kw["scale"] = scale
            if not isinstance(scale, (int, float)):
                reads.append(scale)
        writes = [out]
        if accum_out is not None:
            kw["accum_out"] = accum_out
            writes.append(accum_out)

        def fn(e):
            return e.activation(out, in_, func, **kw)
        return self.add(eng, fn, reads=reads, writes=writes)

    def tt(self, out, in0, in1, op, eng="dve"):
        def fn(e):
            return e.tensor_tensor(out, in0, in1, op)
        return self.add(eng, fn, reads=[in0, in1], writes=[out])

    def ts(self, out, in0, s1, s2, op0, op1=None, eng="dve", accum_out=None):
        reads = [in0]
        if not isinstance(s1, (int, float)):
            reads.append(s1)
        if s2 is not None and not isinstance(s2, (int, float)):
            reads.append(s2)
        writes = [out]
        kw = {}
        if accum_out is not None:
            kw["accum_out"] = accum_out
            writes.append(accum_out)

        def fn(e):
            if op1 is None:
                return e.tensor_scalar(out, in0, s1, None, op0, **kw)
            return e.tensor_scalar(out, in0, s1, s2, op0, op1, **kw)
        return self.add(eng, fn, reads=reads, writes=writes)

    def stt(self, out, in0, scalar, in1, op0, op1, eng="dve"):
        reads = [in0, in1]
        if not isinstance(scalar, (int, float)):
            reads.append(scalar)

        def fn(e):
            return e.scalar_tensor_tensor(out, in0, scalar, in1, op0, op1)
        return self.add(eng, fn, reads=reads, writes=[out])

    def copy(self, out, in_, eng="dve"):
        if eng == "act":
            def fn(e):
                return e.activation(out, in_, AF.Identity)
        else:
            def fn(e):
                return e.tensor_copy(out, in_)
        return self.add(eng, fn, reads=[in_], writes=[out])

    def memset(self, ap, val, eng="dve"):
        def fn(e):
            return e.memset(ap, val)
        return self.add(eng, fn, writes=[ap])

    def recip(self, out, in_):
        def fn(e):
            return e.reciprocal(out, in_)
        return self.add("dve", fn, reads=[in_], writes=[out])

    def emit(self):
        nc = self.nc
        for e in ("pe", "act", "dve", "pool"):
            c = 0
            for op in self.ops[e]:
                if op.isdma:
                    continue
                if op.inc:
                    c += 1
                    op.val = c
        prog = self

        def run(engname, eobj):
            seen = {}
            for op in prog.ops[engname]:
                need = {}
                for d in op.deps:
                    key = ("d", d.sem) if d.isdma else ("e", d.eng)
                    if d.val is None:
                        raise RuntimeError("dep without value")
                    if need.get(key, 0) < d.val:
                        need[key] = d.val
                for key, v in need.items():
                    if seen.get(key, 0) >= v:
                        continue
                    seen[key] = v
                    sem = prog.dsem[key[1]] if key[0] == "d" else prog.esem[key[1]]
                    eobj.wait_ge(sem, v)
                r = op.fn(eobj)
                if op.isdma:
                    for ins in r:
                        ins.then_inc(prog.dsem[op.sem], 16)
                elif op.inc:
                    r.then_inc(prog.esem[engname], 1)
            fin = {}
            for op in prog.alldma:
                if op.eng == engname:
                    fin[op.sem] = max(fin.get(op.sem, 0), op.val)
            for s, v in fin.items():
                if seen.get(("d", s), 0) < v:
                    eobj.wait_ge(prog.dsem[s], v)

        with nc.Block() as block:
            @block.tensor
            def _(e):
                run("pe", e)

            @block.scalar
            def _(e):
                run("act", e)

            @block.vector
            def _(e):
                run("dve", e)

            @block.gpsimd
            def _(e):
                run("pool", e)

            @block.sync
            def _(e):
                run("sp", e)


class Arena:
    def __init__(self, nc, stack, nbytes, name="arena"):
        self.t = stack.enter_context(nc.sbuf_tensor(name, [128, nbytes // 4], F32))
        self.n = nbytes
        self.top = 0
        self.marks = []

    def alloc(self, shape_free, dtype, parts=128):
        sz = DTSIZE[dtype]
        n = int(np.prod(shape_free))
        nb = (n * sz + 31) // 32 * 32
        if self.top + nb > self.n:
            raise RuntimeError("arena overflow: need %d have %d" % (self.top + nb, self.n))
        w0 = self.top // 4
        self.top += nb
        v = self.t[0:parts, w0:w0 + nb // 4]
        if dtype != F32:
            v = v.bitcast(dtype)
        v = v[:, 0:n]
        if len(shape_free) == 2:
            v = v.rearrange("p (a b) -> p a b", a=shape_free[0])
        elif len(shape_free) == 3:
            v = v.rearrange("p (a b c) -> p a b c", a=shape_free[0], b=shape_free[1])
        return v

    def mark(self):
        self.marks.append(self.top)

    def release(self):
        self.top = self.marks.pop()

from concourse.bass_utils import run_bass_kernel_spmd

D = 4096
SEQ = 4096
TB = 1024
NBLK_W = 102
ALPHA = (2 * 2) ** 0.25
LN_EPS = 1e-5
TWO_PI = 6.283185307179586


def _win_cols():
    o = {}
    names = ["pu", "pg", "gq", "gk", "gv", "gg", "glr", "sq", "sk", "sv", "sg", "xq", "xg", "mg"]
    sizes = [1024, 1024, 512, 512, 1024, 1024, 16, 1024, 128, 128, 1024, 1024, 1024, 16384]
    c = 0
    for n, s in zip(names, sizes):
        o[n] = c
        c += s
    cols = []
    blk = {}

    def put(name, lst):
        assert len(lst) % 256 == 0, name
        blk[name] = len(cols) // 256
        cols.extend(lst)
    r = lambda n, a, b: list(range(o[n] + a, o[n] + b))
    put("b0", r("glr", 0, 16) + r("sv", 0, 128) + r("glr", 0, 16) * 7)
    put("gk", r("gk", 0, 512))
    put("gq", r("gq", 0, 512))
    put("gv", r("gv", 0, 1024))
    put("gg", r("gg", 0, 1024))
    put("sq", r("sq", 0, 1024))
    put("sk", r("sk", 0, 64) * 2 + r("sk", 64, 128) * 2)
    put("sg", r("sg", 0, 1024))
    put("xq", r("xq", 0, 1024))
    put("xg", r("xg", 0, 1024))
    put("pu", r("pu", 0, 1024))
    put("pg", r("pg", 0, 1024))
    mg = []
    for dc in range(32):
        for b in range(4):
            mg += r("mg", b * 4096 + dc * 128, b * 4096 + dc * 128 + 128)
    put("mg", mg)
    assert len(cols) == NBLK_W * 256
    return np.array(cols), blk


class _Stop(Exception):
    pass


def build(n_layers=2, n_blocks=4, debug=False, stop=0):
    nc = bass.Bass("TRN2", target_bir_lowering=False)
    _, WB = _win_cols()
    NT = n_blocks * TB

    def din(name, shape, dt=F32):
        return nc.dram_tensor(name, list(shape), dt, kind="ExternalInput").ap()
    x_in = din("x", [SEQ, D])
    mem_in = din("mem", [256, D])
    pos_in = din("pos", [1, SEQ], I32)
    win_in = din("win", [2, NBLK_W, 128, 32 * 256])
    wmk_in = din("wmk", [2, 8, 128, 32 * 256])
    wbr_in = din("wbr", [2, 64, 128, 2 * 8 * 128])
    wout_in = din("wout", [2, 16, 128, 32 * 256])
    wpool_in = din("wpool", [2, 128, 4 * 2 * 256])
    wup_in = din("wup", [2, 17, 512])
    pscale_in = din("pscale", [2, 128, 8])
    gnorm_in = din("gnorm", [2, 1, 1024])
    sink_in = din("sink", [2, 128, 8])
    lng_in = din("lng", [2, 1, D])
    lnb_in = din("lnb", [2, 1, D])
    cst_in = din("cst", [128, 1024])
    y_out = nc.dram_tensor("y", [SEQ, D], F32, kind="ExternalOutput").ap()
    x1_d = nc.dram_tensor("x1s", [SEQ, D], F32).ap()
    v_d = nc.dram_tensor("vs", [TB, D], F32).ap()
    yT_d = nc.dram_tensor("yTs", [32, 128, TB], BF16).ap()
    oT_d = nc.dram_tensor("oTs", [4, 8, 128, TB], BF16).ap()
    dbg_out = None
    if debug:
        dbg_out = nc.dram_tensor("dbg", [4, 8, 128, TB], F32, kind="ExternalOutput").ap()

    stack = ExitStack()
    with stack:
        P = Prog(nc, stack)

        def SC(n):
            if stop == n:
                raise _Stop()
        A = Arena(nc, stack, 206 * 1024)
        pss = [stack.enter_context(nc.psum_tensor("ps%d" % i, [128, 512], F32)) for i in range(8)]
        psi = [0]

        def PS():
            psi[0] = (psi[0] + 1) % 8
            return pss[psi[0]]
        rr = [0]

        def EV():
            rr[0] ^= 1
            return "act" if rr[0] else "dve"

        cst = A.alloc([1024], F32)
        P.dma("sp", cst, cst_in)
        ident = cst[:, 0:128]
        tri = cst[:, 128:256]
        suff = cst[:, 256:384]
        Rm = cst[:, 384:512]
        invf = cst[:, 768:769]
        invc16 = cst[:, 832:896]
        cb = A.alloc([384], BF16)
        P.dma("pool", cb[:, 0:128], cst_in[:, 0:128])
        P.dma("pool", cb[:, 128:384], cst_in[:, 512:768])
        identb = cb[:, 0:128]
        msame = cb[:, 128:256]
        mnext = cb[:, 256:384]
        onesb = A.alloc([128], BF16)
        P.memset(onesb, 1.0)
        S = A.alloc([4, 256], F32)
        Sbf = A.alloc([4, 256], BF16)
        kTh = A.alloc([2, 128], BF16)
        vh = A.alloc([128], BF16)
        puh = A.alloc([8, 16], F32)
        wsl = [A.alloc([32 * 256], BF16) for _ in range(2)]
        wsi = [0]

        def wload(src):
            wsi[0] ^= 1
            n = src.shape[-1]
            v = wsl[wsi[0]][:, 0:n]
            P.dma("pool", v, src)
            return v

        def w3(v, kc, ncol):
            return v.rearrange("p (k c) -> p k c", k=kc)

        try:
          for layer in range(n_layers):
              src_x = x_in if layer == 0 else x1_d
              dst_x = y_out if layer == n_layers - 1 else x1_d
              L = layer
              for h in range(4):
                  P.memset(S[:, h, :], 0.0)
                  P.memset(Sbf[:, h, :], 0.0)
              P.memset(puh, 0.0)
              P.memset(kTh, 0.0)
              P.memset(vh, 0.0)

              for blk in range(n_blocks):
                  t0 = blk * TB
                  first = (blk == 0)
                  A.mark()
                  xT = A.alloc([32, TB], BF16)
                  A.mark()
                  xt = [A.alloc([D], F32) for _ in range(2)]
                  for tt in range(8):
                      xtile = xt[tt % 2]
                      P.dma("sp", xtile, src_x[t0 + tt * 128:t0 + (tt + 1) * 128, :])
                      for g in range(8):
                          ps = PS()
                          for q in range(4):
                              kc = g * 4 + q
                              P.mm(ps[:, q * 128:(q + 1) * 128], [(xtile[:, kc * 128:(kc + 1) * 128], ident)])
                          P.copy(xT[:, g * 4:(g + 1) * 4, tt * 128:(tt + 1) * 128],
                                 ps[:, 0:512].rearrange("p (a b) -> p a b", a=4), eng=EV())
                  A.release()

                  A.mark()
                  wup = A.alloc([512], F32, parts=17)
                  P.dma("sp", wup, wup_in[L])
                  pscale = A.alloc([8], F32)
                  P.dma("sp", pscale, pscale_in[L])
                  gnb = A.alloc([1024], F32)
                  P.dma("sp", gnb, gnorm_in[L].broadcast_to([128, 1024]))
                  sinke = A.alloc([8], F32)
                  P.dma("sp", sinke, sink_in[L])
                  P.act(sinke, sinke, AF.Exp)

                  SC(1)
                  def wblk(name, i=0):
                      return w3(wload(win_in[L, WB[name] + i]), 32, 256)

                  def proj_fm(w, c0, ncol, half):
                      ps = PS()
                      P.mm(ps[0:ncol, 0:512], [(w[:, kc, c0:c0 + ncol], xT[:, kc, half * 512:(half + 1) * 512]) for kc in range(32)])
                      return ps[0:ncol, 0:512]

                  def proj_tm(w, c0, ncol, tt):
                      ps = PS()
                      P.mm(ps[:, 0:ncol], [(xT[:, kc, tt * 128:(tt + 1) * 128], w[:, kc, c0:c0 + ncol]) for kc in range(32)])
                      return ps[:, 0:ncol]

                  VS = A.alloc([9, 128], BF16)
                  stg = [A.alloc([512], BF16) for _ in range(4)]
                  sti = [0]

                  def ostage():
                      sti[0] = (sti[0] + 1) % 4
                      return stg[sti[0]]

                  def oflush(t, b, c, half):
                      P.dma("sp", oT_d[b, c, :, half * 512:(half + 1) * 512], t)
                  A.mark()
                  SP_ = A.alloc([8, 512], F32)
                  QT = A.alloc([4, TB], BF16)
                  KT = A.alloc([4, TB], BF16)
                  KTM = A.alloc([8, 512], BF16)
                  V = A.alloc([8, 1024], BF16)
                  GG = A.alloc([8, 1024], BF16)
                  LRT = A.alloc([TB], F32, parts=17)
                  tmpA = A.alloc([512], F32)
                  tmpB = A.alloc([512], F32)
                  P.memset(LRT, 1.0)
                  w = wblk("b0")
                  for half in range(2):
                      pv = proj_fm(w, 0, 16, half)
                      P.copy(LRT[0:16, half * 512:(half + 1) * 512], pv, eng="dve")
                  P.copy(VS[:, 0, :], vh, eng="dve")
                  for tt in range(8):
                      pv = proj_tm(w, 16, 128, tt)
                      P.copy(VS[:, tt + 1, :], pv, eng=EV())
                  P.copy(vh, VS[:, 8, :], eng="dve")
                  for i in range(2):
                      w = wblk("gk", i)
                      for j in range(2):
                          for half in range(2):
                              pv = proj_fm(w, j * 128, 128, half)
                              P.copy(KT[:, i * 2 + j, half * 512:(half + 1) * 512], pv, eng=EV())
                      for tt in range(8):
                          pv = proj_tm(w, 0, 256, tt)
                          P.copy(KTM[:, tt, i * 256:(i + 1) * 256], pv, eng=EV())
                  for i in range(2):
                      w = wblk("gq", i)
                      for j in range(2):
                          for half in range(2):
                              pv = proj_fm(w, j * 128, 128, half)
                              P.copy(QT[:, i * 2 + j, half * 512:(half + 1) * 512], pv, eng=EV())
                  for i in range(4):
                      w = wblk("gv", i)
                      for tt in range(8):
                          pv = proj_tm(w, 0, 256, tt)
                          P.copy(V[:, tt, i * 256:(i + 1) * 256], pv, eng=EV())
                  for i in range(4):
                      w = wblk("gg", i)
                      for tt in range(8):
                          pv = proj_tm(w, 0, 256, tt)
                          P.act(GG[:, tt, i * 256:(i + 1) * 256], pv, AF.Silu)
                  SC(2)
                  for tt in range(8):
                      ps = PS()
                      P.mm(ps[:, 0:512], [(LRT[0:17, tt * 128:(tt + 1) * 128], wup[0:17, :])])
                      P.act(tmpA, ps[:, 0:512], AF.Exp, scale=-1.0)
                      P.act(SP_[:, tt, :], tmpA, AF.Ln, bias=1.0)
                      ps = PS()
                      P.mm(ps[:, 0:512], [(suff, SP_[:, tt, :])])
                      P.act(tmpB, ps[:, 0:512], AF.Exp, scale=-1.0 / 16)
                      P.tt(KTM[:, tt, :], KTM[:, tt, :], tmpB, ALU.mult)
                  SC(3)
                  eb = A.alloc([128], F32)
                  enb = A.alloc([128], F32)
                  dec = A.alloc([1], F32)
                  ssq = A.alloc([1], F32)
                  rs = A.alloc([1], F32)
                  QE = A.alloc([128], BF16)
                  KN = A.alloc([128], BF16)
                  AT = A.alloc([128], BF16)
                  tmpO = A.alloc([256], F32)
                  junk = A.alloc([256], F32)
                  for tt in range(8):
                      tsl = slice(tt * 128, (tt + 1) * 128)
                      for h in range(4):
                          psA = PS()
                          P.mm(psA[:, 0:128], [(SP_[:, tt, h * 128:(h + 1) * 128], tri)])
                          P.act(eb, psA[:, 0:128], AF.Exp, scale=-1.0 / 16)
                          P.act(enb, psA[:, 0:128], AF.Exp, scale=1.0 / 16)
                          P.act(dec, psA[:, 127:128], AF.Exp, scale=-1.0 / 16)
                          P.stt(QE, QT[:, h, tsl], 128 ** -0.5, eb, ALU.mult, ALU.mult)
                          P.tt(KN, KT[:, h, tsl], enb, ALU.mult)
                          psB = PS()
                          P.mm(psB[:, 0:128], [(KN, QE)])
                          P.tt(AT, psB[:, 0:128], tri, ALU.mult)
                          psC = PS()
                          P.mm(psC[:, 0:256], [(AT, V[:, tt, h * 256:(h + 1) * 256]), (QE, Sbf[:, h, :])])
                          P.copy(tmpO, psC[:, 0:256], eng="act")
                          P.memset(ssq, 0.0)
                          P.act(junk, tmpO, AF.Square, accum_out=ssq)
                          P.act(rs, ssq, AF.Sqrt, bias=LN_EPS, scale=1.0 / 256)
                          P.recip(rs, rs)
                          P.stt(tmpO, tmpO, rs, gnb[:, h * 256:(h + 1) * 256], ALU.mult, ALU.mult)
                          P.tt(GG[:, tt, h * 256:(h + 1) * 256], GG[:, tt, h * 256:(h + 1) * 256], tmpO, ALU.mult)
                          psD = PS()
                          P.mm(psD[:, 0:256], [(KTM[:, tt, h * 128:(h + 1) * 128], V[:, tt, h * 256:(h + 1) * 256])])
                          P.stt(S[:, h, :], S[:, h, :], dec, psD[:, 0:256], ALU.mult, ALU.add)
                          P.copy(Sbf[:, h, :], S[:, h, :], eng="dve")
                      for g in range(2):
                          ps = PS()
                          for q in range(4):
                              c = g * 4 + q
                              P.mm(ps[:, q * 128:(q + 1) * 128], [(GG[:, tt, c * 128:(c + 1) * 128], identb)])
                          st_ = ostage()
                          P.copy(st_, ps[:, 0:512], eng=EV())
                          P.dma("sp", oT_d[1, g * 4:(g + 1) * 4, :, tsl].rearrange("c p t -> p c t"), st_.rearrange("p (a b) -> p a b", a=4))

                  A.release()
                  SC(4)
                  A.mark()
                  QR = A.alloc([8, TB], BF16)
                  KTs = A.alloc([2, 128 + TB], BF16)
                  cosT = A.alloc([TB], F32)
                  sinT = A.alloc([TB], F32)
                  posi = A.alloc([TB], I32)
                  ang = A.alloc([TB], F32)
                  angi = A.alloc([TB], I32)
                  m1 = A.alloc([TB], F32)
                  qf = A.alloc([512], F32)
                  t1 = A.alloc([512], F32)
                  P.dma("sp", posi, pos_in[:, t0:t0 + TB].broadcast_to([128, TB]))
                  P.copy(ang, posi, eng="dve")
                  P.ts(ang, ang, invf, None, ALU.mult)
                  for which, dst in ((0, sinT), (1, cosT)):
                      if which == 1:
                          P.ts(ang, ang, 0.25, None, ALU.add)
                      P.copy(angi, ang, eng="dve")
                      P.copy(m1, angi, eng="dve")
                      P.tt(dst, ang, m1, ALU.subtract)
                      P.ts(m1, dst, 0.5, None, ALU.is_gt)
                      P.tt(dst, dst, m1, ALU.subtract)
                      P.ts(m1, dst, -0.5, None, ALU.is_lt)
                      P.tt(dst, dst, m1, ALU.add)
                      P.act(dst, dst, AF.Sin, scale=TWO_PI)

                  def rope(pv, half, dst):
                      hs = slice(half * 512, (half + 1) * 512)
                      P.copy(qf, pv, eng="act")
                      ps2 = PS()
                      P.mm(ps2[:, 0:512], [(Rm, qf)])
                      P.tt(t1, ps2[:, 0:512], sinT[:, hs], ALU.mult)
                      P.tt(qf, qf, cosT[:, hs], ALU.mult)
                      P.tt(dst, qf, t1, ALU.add)
                  for i in range(4):
                      w = wblk("sq", i)
                      for j in range(2):
                          for half in range(2):
                              pv = proj_fm(w, j * 128, 128, half)
                              rope(pv, half, QR[:, i * 2 + j, half * 512:(half + 1) * 512])
                  w = wblk("sk")
                  for kv in range(2):
                      P.copy(KTs[:, kv, 0:128], kTh[:, kv, :], eng="dve")
                      for half in range(2):
                          pv = proj_fm(w, kv * 128, 128, half)
                          rope(pv, half, KTs[:, kv, 128 + half * 512:128 + (half + 1) * 512])
                      P.copy(kTh[:, kv, :], KTs[:, kv, TB:TB + 128], eng="dve")
                  SC(5)
                  Pm = A.alloc([9, 512], BF16)
                  et = A.alloc([512], F32)
                  rden = A.alloc([TB], F32)
                  gsw = A.alloc([TB], BF16)
                  for c in range(8):
                      kv = c // 4
                      wg = wblk("sg", c // 2) if c % 2 == 0 else wg
                      for half in range(2):
                          pv = proj_fm(wg, (c % 2) * 128, 128, half)
                          P.act(gsw[:, half * 512:(half + 1) * 512], pv, AF.Silu)
                      jt0 = 1 if first else 0
                      for jt in range(jt0, 9):
                          j = jt - 1
                          q_lo = max(j, 0) * 128
                          q_hi = min(j + 2, 8) * 128
                          n = q_hi - q_lo
                          off = 0 if j >= 0 else 128
                          for hh in range(2):
                              hp = slice(hh * 64, (hh + 1) * 64)
                              ps = PS()
                              P.mm(ps[:, off:off + n],
                                   [(KTs[hp, kv, jt * 128:(jt + 1) * 128], QR[hp, c, q_lo:q_hi])])
                              P.act(et[:, hh * 256 + off:hh * 256 + off + n], ps[:, off:off + n], AF.Exp, scale=0.125)
                              if j >= 0:
                                  P.tt(Pm[:, jt, hh * 256:hh * 256 + 128], et[:, hh * 256:hh * 256 + 128], msame, ALU.mult)
                              if j < 7:
                                  P.tt(Pm[:, jt, hh * 256 + 128:hh * 256 + 256], et[:, hh * 256 + 128:hh * 256 + 256], mnext, ALU.mult)
                      SC(51)
                      for half in range(2):
                          pso = PS()
                          pss_ = PS()
                          for qi in range(4):
                              qt = half * 4 + qi
                              contrib = []
                              if not (first and qt == 0):
                                  contrib.append((qt, 128))
                              contrib.append((qt + 1, 0))
                              for hh in range(2):
                                  hp = slice(hh * 64, (hh + 1) * 64)
                                  P.mm(pso[hp, qi * 128:(qi + 1) * 128],
                                       [(VS[:, jt, kv * 64:(kv + 1) * 64], Pm[:, jt, hh * 256 + o_:hh * 256 + o_ + 128]) for jt, o_ in contrib])
                                  P.mm(pss_[hp, qi * 128:(qi + 1) * 128],
                                       [(onesb[:, 0:64], Pm[:, jt, hh * 256 + o_:hh * 256 + o_ + 128]) for jt, o_ in contrib])
                          SC(52)
                          hs = slice(half * 512, (half + 1) * 512)
                          P.ts(rden[:, hs], pss_[:, 0:512], sinke[:, c:c + 1], None, ALU.add)
                          P.recip(rden[:, hs], rden[:, hs])
                          P.tt(rden[:, hs], rden[:, hs], pso[:, 0:512], ALU.mult)
                          st_ = ostage()
                          P.tt(st_, rden[:, hs], gsw[:, hs], ALU.mult)
                          oflush(st_, 2, c, half)
                  A.release()
                  SC(6)
                  A.mark()
                  mkT = A.alloc([8, 256], BF16)
                  mv = A.alloc([2, 1024], BF16)
                  A.mark()
                  memT = A.alloc([32, 256], BF16)
                  mtile = A.alloc([D], F32)
                  for mt in range(2):
                      P.dma("sp", mtile, mem_in[mt * 128:(mt + 1) * 128, :])
                      for g in range(8):
                          ps = PS()
                          for q in range(4):
                              kc = g * 4 + q
                              P.mm(ps[:, q * 128:(q + 1) * 128], [(mtile[:, kc * 128:(kc + 1) * 128], ident)])
                          P.copy(memT[:, g * 4:(g + 1) * 4, mt * 128:(mt + 1) * 128],
                                 ps[:, 0:512].rearrange("p (a b) -> p a b", a=4), eng=EV())
                  for b in range(8):
                      w = w3(wload(wmk_in[L, b]), 32, 256)
                      if b < 4:
                          for j in range(2):
                              ps = PS()
                              P.mm(ps[:, 0:256], [(w[:, kc, j * 128:(j + 1) * 128], memT[:, kc, :]) for kc in range(32)])
                              P.copy(mkT[:, b * 2 + j, :], ps[:, 0:256], eng=EV())
                      else:
                          for mt in range(2):
                              ps = PS()
                              P.mm(ps[:, 0:256], [(memT[:, kc, mt * 128:(mt + 1) * 128], w[:, kc, :]) for kc in range(32)])
                              P.copy(mv[:, mt, (b - 4) * 256:(b - 3) * 256], ps[:, 0:256], eng=EV())
                  A.release()

                  Qm = A.alloc([2, TB], BF16)
                  E = A.alloc([2, 512], BF16)
                  rsum = A.alloc([512], F32)
                  gm = A.alloc([512], F32)
                  for h in range(4):
                      wq = wblk("xq", h)
                      for dc in range(2):
                          for half in range(2):
                              pv = proj_fm(wq, dc * 128, 128, half)
                              P.copy(Qm[:, dc, half * 512:(half + 1) * 512], pv, eng=EV())
                      wgm = wblk("xg", h)
                      for half in range(2):
                          hs = slice(half * 512, (half + 1) * 512)
                          for mt in range(2):
                              ps = PS()
                              P.mm(ps[:, 0:512], [(mkT[:, h * 2 + dc, mt * 128:(mt + 1) * 128], Qm[:, dc, hs]) for dc in range(2)])
                              P.act(E[:, mt, :], ps[:, 0:512], AF.Exp, scale=1.0 / 16)
                          ps = PS()
                          P.mm(ps[:, 0:512], [(onesb, E[:, mt, :]) for mt in range(2)])
                          P.recip(rsum, ps[:, 0:512])
                          for dc in range(2):
                              pg_ = proj_fm(wgm, dc * 128, 128, half)
                              P.act(gm, pg_, AF.Silu)
                              P.tt(gm, gm, rsum, ALU.mult)
                              ps = PS()
                              P.mm(ps[:, 0:512], [(mv[:, mt, h * 256 + dc * 128:h * 256 + (dc + 1) * 128], E[:, mt, :]) for mt in range(2)])
                              st_ = ostage()
                              P.tt(st_, ps[:, 0:512], gm, ALU.mult)
                              oflush(st_, 3, h * 2 + dc, half)
                  A.release()
                  SC(7)
                  A.mark()
                  wpool = A.alloc([4, 2, 256], BF16)
                  P.dma("pool", wpool, wpool_in[L].rearrange("p (g j d) -> p g j d", g=4, j=2))
                  U = A.alloc([2, 16 + TB], F32)
                  Sa = A.alloc([16 + TB], F32)
                  Sb = A.alloc([16 + TB], F32)
                  Pb = A.alloc([2, TB], BF16)
                  gp = A.alloc([512], F32)
                  for g in range(4):
                      wdw = 2 ** (g + 1)
                      wu = wblk("pu", g)
                      for j in range(2):
                          P.copy(U[:, j, 0:16], puh[:, g * 2 + j, :], eng="dve")
                          for half in range(2):
                              pv = proj_fm(wu, j * 128, 128, half)
                              P.copy(U[:, j, 16 + half * 512:16 + (half + 1) * 512], pv, eng=EV())
                          P.copy(puh[:, g * 2 + j, :], U[:, j, TB:TB + 16], eng="dve")
                          cur = U[:, j, :]
                          bufs = [Sa, Sb]
                          lo = 0
                          for k in range(g + 1):
                              sh = 2 ** k
                              nxt = bufs[k % 2]
                              lo2 = lo + sh
                              P.tt(nxt[:, lo2:16 + TB], cur[:, lo2:16 + TB], cur[:, lo2 - sh:16 + TB - sh], ALU.add)
                              cur = nxt
                              lo = lo2
                          P.stt(Pb[:, j, :], cur[:, 16:16 + TB], 1.0 / wdw, U[:, j, 16:16 + TB], ALU.mult, ALU.subtract)
                          if first:
                              P.tt(Sa[:, 0:16] if cur is not Sa else Sb[:, 0:16], cur[:, 16:32], invc16[:, g * 16:(g + 1) * 16], ALU.mult)
                              tmp16 = Sa[:, 0:16] if cur is not Sa else Sb[:, 0:16]
                              P.tt(Pb[:, j, 0:16], tmp16, U[:, j, 16:32], ALU.subtract)
                      wgp = wblk("pg", g)
                      for dj in range(2):
                          for half in range(2):
                              hs = slice(half * 512, (half + 1) * 512)
                              pg_ = proj_fm(wgp, dj * 128, 128, half)
                              P.act(gp, pg_, AF.Silu)
                              ps = PS()
                              P.mm(ps[:, 0:512], [(wpool[:, g, cj, dj * 128:(dj + 1) * 128], Pb[:, cj, hs]) for cj in range(2)])
                              st_ = ostage()
                              P.stt(st_, ps[:, 0:512], pscale[:, g * 2 + dj:g * 2 + dj + 1], gp, ALU.mult, ALU.mult)
                              oflush(st_, 0, g * 2 + dj, half)
                  A.release()
                  SC(8)
                  A.release()
                  oT = A.alloc([4, 8, TB], BF16)
                  for b in range(4):
                      P.dma("sp", oT[:, b, :, :], oT_d[b].rearrange("c p t -> p c t"))
                  if debug:
                      A.mark()
                      dtile = A.alloc([TB], F32)
                      for b in range(4):
                          for c in range(8):
                              P.copy(dtile, oT[:, b, c, :], eng="dve")
                              P.dma("sp", dbg_out[b, c], dtile)
                      A.release()
                  A.mark()
                  wbs = [A.alloc([2, 8, 128], BF16) for _ in range(2)]
                  yacc = A.alloc([TB], F32)
                  sg_ = A.alloc([512], F32)
                  ybf = A.alloc([TB], BF16)
                  for dc in range(32):
                      for pr in range(2):
                          w = w3(wload(win_in[L, WB["mg"] + dc * 2 + pr]), 32, 256)
                          wb_ = wbs[pr]
                          P.dma("pool", wb_, wbr_in[L, dc * 2 + pr].rearrange("p (b k c) -> p b k c", b=2, k=8))
                          for bi in range(2):
                              b = pr * 2 + bi
                              for half in range(2):
                                  hs = slice(half * 512, (half + 1) * 512)
                                  pv = proj_fm(w, bi * 128, 128, half)
                                  P.act(sg_, pv, AF.Sigmoid)
                                  ps = PS()
                                  P.mm(ps[:, 0:512], [(wb_[:, bi, kc, :], oT[:, b, kc, hs]) for kc in range(8)])
                                  if b == 0:
                                      P.tt(yacc[:, hs], ps[:, 0:512], sg_, ALU.mult)
                                  else:
                                      P.tt(sg_, ps[:, 0:512], sg_, ALU.mult)
                                      P.tt(yacc[:, hs], yacc[:, hs], sg_, ALU.add)
                      P.copy(ybf, yacc, eng="act")
                      P.dma("sp", yT_d[dc], ybf)
                  A.release()
                  A.release()
                  SC(9)
                  A.mark()
                  sums = A.alloc([8, 16], F32)
                  sqs = A.alloc([8, 16], F32)
                  P.memset(sums, 0.0)
                  P.memset(sqs, 0.0)
                  A.mark()
                  yT = A.alloc([32, TB], BF16)
                  P.dma("sp", yT, yT_d.rearrange("c p t -> p c t"))
                  xb = [A.alloc([256], F32) for _ in range(2)]
                  vb = [A.alloc([256], F32) for _ in range(2)]
                  jk = A.alloc([256], F32)
                  k_ = 0
                  for cbk in range(16):
                      w = w3(wload(wout_in[L, cbk]), 32, 256)
                      for tt in range(8):
                          xx = xb[k_ % 2]
                          vv = vb[k_ % 2]
                          k_ += 1
                          P.dma("sp", xx, src_x[t0 + tt * 128:t0 + (tt + 1) * 128, cbk * 256:(cbk + 1) * 256])
                          ps = PS()
                          P.mm(ps[:, 0:256], [(yT[:, kc, tt * 128:(tt + 1) * 128], w[:, kc, :]) for kc in range(32)])
                          P.stt(vv, xx, ALPHA, ps[:, 0:256], ALU.mult, ALU.add)
                          P.act(jk, vv, AF.Identity, accum_out=sums[:, tt, cbk:cbk + 1])
                          P.act(jk, vv, AF.Square, accum_out=sqs[:, tt, cbk:cbk + 1])
                          P.dma("sp", v_d[tt * 128:(tt + 1) * 128, cbk * 256:(cbk + 1) * 256], vv)
                  A.release()
                  gb = A.alloc([D], F32)
                  bb = A.alloc([D], F32)
                  P.dma("sp", gb, lng_in[L].broadcast_to([128, D]))
                  P.dma("sp", bb, lnb_in[L].broadcast_to([128, D]))
                  vt = [A.alloc([D], F32) for _ in range(2)]
                  mean = A.alloc([8], F32)
                  msq = A.alloc([8], F32)
                  rstd = A.alloc([8], F32)
                  jk16 = A.alloc([16], F32)

                  def red(dst, src):
                      def fn(e):
                          return e.reduce_sum(dst, src, AX.X)
                      return P.add("dve", fn, reads=[src], writes=[dst])
                  for tt in range(8):
                      c1 = slice(tt, tt + 1)
                      P.memset(mean[:, c1], 0.0)
                      P.memset(msq[:, c1], 0.0)
                      P.act(jk16, sums[:, tt, :], AF.Identity, accum_out=mean[:, c1])
                      P.act(jk16, sqs[:, tt, :], AF.Identity, accum_out=msq[:, c1])
                      P.ts(mean[:, c1], mean[:, c1], 1.0 / D, None, ALU.mult)
                      P.ts(msq[:, c1], msq[:, c1], 1.0 / D, None, ALU.mult)
                      P.tt(rstd[:, c1], mean[:, c1], mean[:, c1], ALU.mult)
                      P.tt(rstd[:, c1], msq[:, c1], rstd[:, c1], ALU.subtract)
                      P.act(rstd[:, c1], rstd[:, c1], AF.Sqrt, bias=LN_EPS)
                      P.recip(rstd[:, c1], rstd[:, c1])
                      v_ = vt[tt % 2]
                      P.dma("sp", v_, v_d[tt * 128:(tt + 1) * 128, :])
                      P.ts(v_, v_, mean[:, c1], rstd[:, c1], ALU.subtract, ALU.mult)
                      P.tt(v_, v_, gb, ALU.mult)
                      P.tt(v_, v_, bb, ALU.add)
                      P.dma("sp", dst_x[t0 + tt * 128:t0 + (tt + 1) * 128, :], v_)
                  A.release()
        except _Stop:
            pass
        P.emit()
    return nc


def _prep(inputs):
    cols, WB = _win_cols()
    f = np.float32
    w_in = np.asarray(inputs["w_in"], f)
    win = np.empty((2, NBLK_W, 128, 32 * 256), f)
    for l in range(2):
        wp = w_in[l][:, cols]
        win[l] = wp.reshape(32, 128, NBLK_W, 256).transpose(2, 1, 0, 3).reshape(NBLK_W, 128, 32 * 256)
    wmk = np.asarray(inputs["w_mem_kv"], f).reshape(2, 32, 128, 8, 256).transpose(0, 3, 2, 1, 4).reshape(2, 8, 128, 32 * 256)
    wout = np.asarray(inputs["w_out"], f).reshape(2, 32, 128, 16, 256).transpose(0, 3, 2, 1, 4).reshape(2, 16, 128, 32 * 256)
    wb = np.asarray(inputs["w_branch"], f).reshape(2, 2, 2, 8, 128, 32, 128)
    wbr = wb.transpose(0, 5, 1, 4, 2, 3, 6).reshape(2, 64, 128, 2 * 8 * 128)
    wpool = np.asarray(inputs["w_pool"], f).reshape(2, 4, 2, 128, 256).transpose(0, 3, 1, 2, 4).reshape(2, 128, 4 * 2 * 256)
    wup = np.concatenate([np.asarray(inputs["w_gla_up"], f), np.asarray(inputs["b_gla"], f)[:, None, :]], axis=1)
    pscale = np.asarray(inputs["pool_scale"], f).reshape(2, 8, 128).transpose(0, 2, 1)
    gnorm = np.asarray(inputs["gla_norm"], f).reshape(2, 1, 1024)
    sk = np.asarray(inputs["sinks"], f)
    sink = np.repeat(sk.reshape(2, 8, 2), 64, axis=2).transpose(0, 2, 1)
    lng = np.asarray(inputs["ln_g"], f).reshape(2, 1, D)
    lnb = np.asarray(inputs["ln_b"], f).reshape(2, 1, D)
    cst = np.zeros((128, 1024), f)
    j = np.arange(128)
    cst[:, 0:128] = np.eye(128)
    cst[:, 128:256] = (j[:, None] <= j[None, :])
    cst[:, 256:384] = (j[:, None] > j[None, :])
    R = np.zeros((128, 128), f)
    for p in range(128):
        jj = p % 64
        if jj < 8:
            R[p + 8, p] = -1.0
        elif jj < 16:
            R[p - 8, p] = 1.0
    cst[:, 384:512] = R
    cst[:, 512:640] = (j[None, :] >= j[:, None])
    cst[:, 640:768] = (j[None, :] < j[:, None])
    jj = j % 64
    inv = np.where(jj < 16, 500000.0 ** (-((jj % 8) * 2.0) / 16.0), 0.0)
    cst[:, 768] = (inv / TWO_PI)
    for g, wd in enumerate((2, 4, 8, 16)):
        cst[:, 832 + g * 16:832 + (g + 1) * 16] = 1.0 / np.minimum(np.arange(16) + 1, wd)
    shared = dict(win=win, wmk=np.ascontiguousarray(wmk), wbr=np.ascontiguousarray(wbr), wout=np.ascontiguousarray(wout),
                  wpool=np.ascontiguousarray(wpool), wup=np.ascontiguousarray(wup), pscale=np.ascontiguousarray(pscale),
                  gnorm=gnorm, sink=np.ascontiguousarray(sink), lng=lng, lnb=lnb, cst=cst)
    return shared


def kernel(**inputs):
    shared = _prep(inputs)
    x = np.asarray(inputs["x"], np.float32)
    mem = np.asarray(inputs["mem"], np.float32)
    pos = np.asarray(inputs["positions"], np.int32)
    nc = build()
    in_maps = []
    for b in range(2):
        m = dict(shared)
        m["x"] = np.ascontiguousarray(x[b])
        m["mem"] = np.ascontiguousarray(mem[b])
        m["pos"] = np.ascontiguousarray(pos[b:b + 1])
        in_maps.append(m)
    res = run_bass_kernel_spmd(nc, in_maps, core_ids=[0, 1])
    return np.stack([res.results[b]["y"] for b in range(2)], axis=0).astype(np.float32)
```

```python
import numpy as np
import concourse.bass as bass
import concourse.mybir as mybir
from contextlib import ExitStack

F32 = mybir.dt.float32
BF16 = mybir.dt.bfloat16
I32 = mybir.dt.int32
AF = mybir.ActivationFunctionType
ALU = mybir.AluOpType
AX = mybir.AxisListType
DTSIZE = {F32: 4, BF16: 2, I32: 4}


class Op:
    __slots__ = ("eng", "fn", "deps", "inc", "val", "isdma", "sem", "idx", "tag", "region", "ndma")


def _box(ap):
    name = ap.tensor.name
    sz = DTSIZE[ap.dtype]
    pairs = ap.ap
    off = ap.offset
    sp = str(ap.space)
    if "DRAM" in sp.upper() or "HBM" in sp.upper():
        span = sum((c - 1) * abs(s) for s, c in pairs)
        return (name, 0, 1, off * sz, (off + span + 1) * sz)
    pstep, pcnt = pairs[0]
    if pstep == 0:
        pstep = 1 << 40
    p0 = off // pstep
    f0 = off % pstep
    span = sum((c - 1) * abs(s) for s, c in pairs[1:])
    return (name, p0, p0 + pcnt, f0 * sz, (f0 + span + 1) * sz)


def _ovl(a, b):
    return a[1] < b[2] and b[1] < a[2] and a[3] < b[4] and b[3] < a[4]


def _cov(a, b):
    return a[1] <= b[1] and a[2] >= b[2] and a[3] <= b[3] and a[4] >= b[4]


class Prog:
    ENGS = ("pe", "act", "dve", "pool", "sp")
    NDS = 40
    SAME_ENG_SYNC = True

    def __init__(self, nc, stack):
        self.nc = nc
        self.ops = {e: [] for e in self.ENGS}
        self.acc = {}
        self.esem = {e: stack.enter_context(nc.semaphore("s_" + e)) for e in ("pe", "act", "dve", "pool")}
        self.dsem = [stack.enter_context(nc.semaphore("d%d" % i)) for i in range(self.NDS)]
        self.dlast = [None] * self.NDS
        self.dval = [0] * self.NDS
        self.dnext = 0
        self.alldma = []
        self.regions = []
        self.cur_region = None
        self.rank_ap = None

    def add(self, eng, fn, reads=(), writes=(), dma=0, tag=""):
        op = Op()
        op.eng = eng
        op.fn = fn
        op.inc = False
        op.val = None
        op.isdma = dma
        op.sem = None
        op.tag = tag
        op.region = self.cur_region
        op.ndma = dma
        deps = set()
        rb = [_box(a) for a in reads]
        wb = [_box(a) for a in writes]
        for b in rb:
            for a in self.acc.get(b[0], ()):
                if a[1] and _ovl(a[2], b):
                    deps.add(a[0])
        for b in wb:
            for a in self.acc.get(b[0], ()):
                if _ovl(a[2], b):
                    deps.add(a[0])
        for b in wb:
            lst = self.acc.setdefault(b[0], [])
            lst[:] = [a for a in lst if not _cov(b, a[2])]
            lst.append((op, True, b))
        for b in rb:
            lst = self.acc.setdefault(b[0], [])
            if not dma:
                lst[:] = [a for a in lst if not (a[0].eng == eng and not a[0].isdma and not a[1] and a[2] == b)]
            lst.append((op, False, b))
        deps.discard(op)
        if dma:
            i = self.dnext
            self.dnext = (self.dnext + 1) % self.NDS
            if self.dlast[i] is not None:
                deps.add(self.dlast[i])
            self.dlast[i] = op
            self.dval[i] += 16 * dma
            op.sem = i
            op.val = self.dval[i]
            self.alldma.append(op)
        fdeps = []
        for d in deps:
            if d.isdma:
                fdeps.append(d)
            elif d.eng == eng:
                if eng != "pe" and self.SAME_ENG_SYNC:
                    fdeps.append(d)
                    d.inc = True
            else:
                fdeps.append(d)
                d.inc = True
        op.deps = fdeps
        op.idx = len(self.ops[eng])
        self.ops[eng].append(op)
        return op

    def load_rank(self, rank_sb):
        self.rank_ap = rank_sb
        for e in self.ENGS:
            def fn(eobj, reg, ap=rank_sb):
                return eobj.reg_load(reg, ap)
            self.add(e, fn, reads=[rank_sb], tag="rankload")

    def region_begin(self, kind, val):
        assert self.cur_region is None
        self.regions.append((kind, val))
        self.cur_region = len(self.regions) - 1

    def region_end(self):
        self.cur_region = None

    def dma(self, eng, out, in_, **kw):
        def fn(e):
            return [e.dma_start(out=out, in_=in_, **kw)]
        return self.add(eng, fn, reads=[in_], writes=[out], dma=1)

    def dmas(self, eng, pairs):
        def fn(e):
            return [e.dma_start(out=o, in_=i) for o, i in pairs]
        return self.add(eng, fn, reads=[i for o, i in pairs], writes=[o for o, i in pairs], dma=len(pairs))

    def mm(self, out, pairs, extra_reads=(), **kw):
        n = len(pairs)

        def fn(e):
            ins = None
            for k, (l, r) in enumerate(pairs):
                ins = e.matmul(out, l, r, start=(k == 0), stop=(k == n - 1), **kw)
            return ins
        reads = []
        for l, r in pairs:
            reads.append(l)
            reads.append(r)
        return self.add("pe", fn, reads=reads, writes=[out])

    def transpose(self, out, in_, ident):
        def fn(e):
            return e.transpose(out, in_, ident)
        return self.add("pe", fn, reads=[in_, ident], writes=[out])

    def act(self, out, in_, func, bias=None, scale=None, accum_out=None, eng="act"):
        kw = {}
        reads = [in_]
        if bias is not None:
            kw["bias"] = bias
            if not isinstance(bias, (int, float)):
                reads.append(bias)
        if scale is not None:
            kw["scale"] = scale
            if not isinstance(scale, (int, float)):
                reads.append(scale)
        writes = [out]
        if accum_out is not None:
            kw["accum_out"] = accum_out
            writes.append(accum_out)

        def fn(e):
            return e.activation(out, in_, func, **kw)
        return self.add(eng, fn, reads=reads, writes=writes)

    def tt(self, out, in0, in1, op, eng="dve"):
        def fn(e):
            return e.tensor_tensor(out, in0, in1, op)
        return self.add(eng, fn, reads=[in0, in1], writes=[out])

    def ts(self, out, in0, s1, s2, op0, op1=None, eng="dve", accum_out=None):
        reads = [in0]
        if not isinstance(s1, (int, float)):
            reads.append(s1)
        if s2 is not None and not isinstance(s2, (int, float)):
            reads.append(s2)
        writes = [out]
        kw = {}
        if accum_out is not None:
            kw["accum_out"] = accum_out
            writes.append(accum_out)

        def fn(e):
            if op1 is None:
                return e.tensor_scalar(out, in0, s1, None, op0, **kw)
            return e.tensor_scalar(out, in0, s1, s2, op0, op1, **kw)
        return self.add(eng, fn, reads=reads, writes=writes)

    def stt(self, out, in0, scalar, in1, op0, op1, eng="dve"):
        reads = [in0, in1]
        if not isinstance(scalar, (int, float)):
            reads.append(scalar)

        def fn(e):
            return e.scalar_tensor_tensor(out, in0, scalar, in1, op0, op1)
        return self.add(eng, fn, reads=reads, writes=[out])

    def copy(self, out, in_, eng="dve"):
        if eng == "act":
            def fn(e):
                return e.activation(out, in_, AF.Identity)
        else:
            def fn(e):
                return e.tensor_copy(out, in_)
        return self.add(eng, fn, reads=[in_], writes=[out])

    def memset(self, ap, val, eng="dve"):
        def fn(e):
            return e.memset(ap, val)
        return self.add(eng, fn, writes=[ap])

    def recip(self, out, in_):
        def fn(e):
            return e.reciprocal(out, in_)
        return self.add("dve", fn, reads=[in_], writes=[out])

    def emit(self):
        nc = self.nc
        for e in ("pe", "act", "dve", "pool"):
            c = 0
            for op in self.ops[e]:
                if op.isdma:
                    continue
                if op.inc:
                    c += 1
                    op.val = c
        prog = self

        def run(engname, eobj):
            seen = {}
            ops = prog.ops[engname]
            reg = None
            if prog.rank_ap is not None:
                reg = eobj.alloc_register("rk_" + engname)

            def emit_op(op):
                need = {}
                for d in op.deps:
                    key = ("d", d.sem) if d.isdma else ("e", d.eng)
                    if d.val is None:
                        raise RuntimeError("dep without value")
                    if need.get(key, 0) < d.val:
                        need[key] = d.val
                for key, v in need.items():
                    if seen.get(key, 0) >= v:
                        continue
                    seen[key] = v
                    sem = prog.dsem[key[1]] if key[0] == "d" else prog.esem[key[1]]
                    eobj.wait_ge(sem, v)
                r = op.fn(eobj, reg) if op.tag == "rankload" else op.fn(eobj)
                if op.isdma:
                    for ins in r:
                        ins.then_inc(prog.dsem[op.sem], 16)
                elif op.inc:
                    r.then_inc(prog.esem[engname], 1)

            def fixups(rops, base):
                rid = rops[0].region
                need = {}
                for o in rops:
                    for d in o.deps:
                        if d.region == rid:
                            continue
                        key = ("d", d.sem) if d.isdma else ("e", d.eng)
                        if need.get(key, 0) < d.val:
                            need[key] = d.val
                for key, v in need.items():
                    if seen.get(key, 0) >= v:
                        continue
                    sem = prog.dsem[key[1]] if key[0] == "d" else prog.esem[key[1]]
                    eobj.wait_ge(sem, v)
                n_inc = sum(1 for o in rops if (not o.isdma) and o.inc)
                if n_inc:
                    if base > 0:
                        eobj.wait_ge(prog.esem[engname], base)
                    eobj.sem_inc(prog.esem[engname], n_inc)
                for o in rops:
                    if o.isdma:
                        prev = o.val - 16 * o.ndma
                        if prev > 0:
                            eobj.wait_ge(prog.dsem[o.sem], prev)
                        eobj.sem_inc(prog.dsem[o.sem], 16 * o.ndma)

            i = 0
            n = len(ops)
            while i < n:
                op = ops[i]
                if op.region is None:
                    emit_op(op)
                    i += 1
                    continue
                j = i
                while j < n and ops[j].region == op.region:
                    j += 1
                rops = ops[i:j]
                base = 0
                for o in ops[:i]:
                    if (not o.isdma) and o.inc:
                        base = o.val
                kind, val = prog.regions[op.region]
                snap = dict(seen)
                if kind == "eq":
                    g = eobj.If_eq(reg, val)
                    g.__enter__()
                    for o in rops:
                        emit_op(o)
                    g.__exit__(None, None, None)
                    g = eobj.Else()
                    g.__enter__()
                    fixups(rops, base)
                    g.__exit__(None, None, None)
                else:
                    g = eobj.If_lt(reg, val)
                    g.__enter__()
                    fixups(rops, base)
                    g.__exit__(None, None, None)
                    g = eobj.Else()
                    g.__enter__()
                    for o in rops:
                        emit_op(o)
                    g.__exit__(None, None, None)
                seen.clear()
                seen.update(snap)
                i = j
            fin = {}
            for op in prog.alldma:
                if op.eng == engname:
                    fin[op.sem] = max(fin.get(op.sem, 0), op.val)
            for s_, v in fin.items():
                if seen.get(("d", s_), 0) < v:
                    eobj.wait_ge(prog.dsem[s_], v)

        with nc.Block() as block:
            @block.tensor
            def _(e):
                run("pe", e)

            @block.scalar
            def _(e):
                run("act", e)

            @block.vector
            def _(e):
                run("dve", e)

            @block.gpsimd
            def _(e):
                run("pool", e)

            @block.sync
            def _(e):
                run("sp", e)


class Arena:
    def __init__(self, nc, stack, nbytes, name="arena"):
        self.t = stack.enter_context(nc.sbuf_tensor(name, [128, nbytes // 4], F32))
        self.n = nbytes
        self.top = 0
        self.marks = []

    def alloc(self, shape_free, dtype, parts=128):
        sz = DTSIZE[dtype]
        n = int(np.prod(shape_free))
        nb = (n * sz + 31) // 32 * 32
        if self.top + nb > self.n:
            raise RuntimeError("arena overflow: need %d have %d" % (self.top + nb, self.n))
        w0 = self.top // 4
        self.top += nb
        v = self.t[0:parts, w0:w0 + nb // 4]
        if dtype != F32:
            v = v.bitcast(dtype)
        v = v[:, 0:n]
        if len(shape_free) == 2:
            v = v.rearrange("p (a b) -> p a b", a=shape_free[0])
        elif len(shape_free) == 3:
            v = v.rearrange("p (a b c) -> p a b c", a=shape_free[0], b=shape_free[1])
        return v

    def mark(self):
        self.marks.append(self.top)

    def release(self):
        self.top = self.marks.pop()

from concourse.bass_utils import run_bass_kernel_spmd

D = 4096
SEQ = 4096
TB = 1024
NBLK_W = 102
ALPHA = (2 * 2) ** 0.25
LN_EPS = 1e-5
TWO_PI = 6.283185307179586


def _win_cols():
    o = {}
    names = ["pu", "pg", "gq", "gk", "gv", "gg", "glr", "sq", "sk", "sv", "sg", "xq", "xg", "mg"]
    sizes = [1024, 1024, 512, 512, 1024, 1024, 16, 1024, 128, 128, 1024, 1024, 1024, 16384]
    c = 0
    for n, s in zip(names, sizes):
        o[n] = c
        c += s
    cols = []
    blk = {}

    def put(name, lst):
        assert len(lst) % 256 == 0, name
        blk[name] = len(cols) // 256
        cols.extend(lst)
    r = lambda n, a, b: list(range(o[n] + a, o[n] + b))
    put("b0", r("glr", 0, 16) + r("sv", 0, 128) + r("glr", 0, 16) * 7)
    put("gk", r("gk", 0, 512))
    put("gq", r("gq", 0, 512))
    put("gv", r("gv", 0, 1024))
    put("gg", r("gg", 0, 1024))
    put("sq", r("sq", 0, 1024))
    put("sk", r("sk", 0, 64) * 2 + r("sk", 64, 128) * 2)
    put("sg", r("sg", 0, 1024))
    put("xq", r("xq", 0, 1024))
    put("xg", r("xg", 0, 1024))
    put("pu", r("pu", 0, 1024))
    put("pg", r("pg", 0, 1024))
    mg = []
    for dc in range(32):
        for b in range(4):
            mg += r("mg", b * 4096 + dc * 128, b * 4096 + dc * 128 + 128)
    put("mg", mg)
    assert len(cols) == NBLK_W * 256
    return np.array(cols), blk


class _Stop(Exception):
    pass


def build(n_layers=2, n_blocks=4, debug=False, stop=0, ranked=True):
    nc = bass.Bass("TRN2", target_bir_lowering=False)
    _, WB = _win_cols()
    NT = n_blocks * TB

    def din(name, shape, dt=F32):
        return nc.dram_tensor(name, list(shape), dt, kind="ExternalInput").ap()
    x_in = din("x", [SEQ, D])
    mem_in = din("mem", [256, D])
    pos_in = din("pos", [1, SEQ], I32)
    win_in = din("win", [2, NBLK_W, 128, 32 * 256])
    wmk_in = din("wmk", [2, 8, 128, 32 * 256])
    wbr_in = din("wbr", [2, 64, 128, 2 * 8 * 128])
    wout_in = din("wout", [2, 16, 128, 32 * 256])
    wpool_in = din("wpool", [2, 128, 4 * 2 * 256])
    wup_in = din("wup", [2, 17, 512])
    pscale_in = din("pscale", [2, 128, 8])
    gnorm_in = din("gnorm", [2, 1, 1024])
    sink_in = din("sink", [2, 128, 8])
    lng_in = din("lng", [2, 1, D])
    lnb_in = din("lnb", [2, 1, D])
    cst_in = din("cst", [128, 1024])
    y_out = nc.dram_tensor("y", [TB if ranked else SEQ, D], F32, kind="ExternalOutput").ap()
    rk_in = din("rk", [1, 1], I32)
    x1_d = nc.dram_tensor("x1s", [SEQ, D], F32).ap()
    v_d = nc.dram_tensor("vs", [TB, D], F32).ap()
    yT_d = nc.dram_tensor("yTs", [32, 128, TB], BF16).ap()
    oT_d = nc.dram_tensor("oTs", [4, 8, 128, TB], BF16).ap()
    dbg_out = None
    if debug:
        dbg_out = nc.dram_tensor("dbg", [4, 8, 128, TB], F32, kind="ExternalOutput").ap()

    stack = ExitStack()
    with stack:
        P = Prog(nc, stack)

        def SC(n):
            if stop == n:
                raise _Stop()
        A = Arena(nc, stack, 206 * 1024)
        pss = [stack.enter_context(nc.psum_tensor("ps%d" % i, [128, 512], F32)) for i in range(8)]
        psi = [0]

        def PS():
            psi[0] = (psi[0] + 1) % 8
            return pss[psi[0]]
        rr = [0]

        def EV():
            rr[0] ^= 1
            return "act" if rr[0] else "dve"

        cst = A.alloc([1024], F32)
        P.dma("sp", cst, cst_in)
        ident = cst[:, 0:128]
        tri = cst[:, 128:256]
        suff = cst[:, 256:384]
        Rm = cst[:, 384:512]
        invf = cst[:, 768:769]
        invc16 = cst[:, 832:896]
        cb = A.alloc([384], BF16)
        P.dma("pool", cb[:, 0:128], cst_in[:, 0:128])
        P.dma("pool", cb[:, 128:384], cst_in[:, 512:768])
        identb = cb[:, 0:128]
        msame = cb[:, 128:256]
        mnext = cb[:, 256:384]
        onesb = A.alloc([128], BF16)
        P.memset(onesb, 1.0)
        rks = A.alloc([8], I32, parts=1)
        P.dma("sp", rks[:, 0:1], rk_in)
        if ranked:
            P.load_rank(rks[:, 0:1])
        S = A.alloc([4, 256], F32)
        Sbf = A.alloc([4, 256], BF16)
        kTh = A.alloc([2, 128], BF16)
        vh = A.alloc([128], BF16)
        puh = A.alloc([8, 16], F32)
        wsl = [A.alloc([32 * 256], BF16) for _ in range(2)]
        wsi = [0]

        def wload(src):
            wsi[0] ^= 1
            n = src.shape[-1]
            v = wsl[wsi[0]][:, 0:n]
            P.dma("pool", v, src)
            return v

        def w3(v, kc, ncol):
            return v.rearrange("p (k c) -> p k c", k=kc)

        try:
          for layer in range(n_layers):
              src_x = x_in if layer == 0 else x1_d
              dst_x = y_out if layer == n_layers - 1 else x1_d
              L = layer
              for h in range(4):
                  P.memset(S[:, h, :], 0.0)
                  P.memset(Sbf[:, h, :], 0.0)
              P.memset(puh, 0.0)
              P.memset(kTh, 0.0)
              P.memset(vh, 0.0)


              def state_pass(blk):
                  t0 = blk * TB
                  A.mark()
                  xT = A.alloc([32, TB], BF16)
                  A.mark()
                  xt = [A.alloc([D], F32) for _ in range(2)]
                  for tt in range(8):
                      xtile = xt[tt % 2]
                      P.dma("sp", xtile, src_x[t0 + tt * 128:t0 + (tt + 1) * 128, :])
                      for g in range(8):
                          ps = PS()
                          for q in range(4):
                              kc = g * 4 + q
                              P.mm(ps[:, q * 128:(q + 1) * 128], [(xtile[:, kc * 128:(kc + 1) * 128], ident)])
                          P.copy(xT[:, g * 4:(g + 1) * 4, tt * 128:(tt + 1) * 128],
                                 ps[:, 0:512].rearrange("p (a b) -> p a b", a=4), eng=EV())
                  A.release()

                  def wblk(name, i=0):
                      return w3(wload(win_in[L, WB[name] + i]), 32, 256)

                  def proj_fm(w, c0, ncol, half):
                      ps = PS()
                      P.mm(ps[0:ncol, 0:512], [(w[:, kc, c0:c0 + ncol], xT[:, kc, half * 512:(half + 1) * 512]) for kc in range(32)])
                      return ps[0:ncol, 0:512]

                  def proj_tm(w, c0, ncol, tt):
                      ps = PS()
                      P.mm(ps[:, 0:ncol], [(xT[:, kc, tt * 128:(tt + 1) * 128], w[:, kc, c0:c0 + ncol]) for kc in range(32)])
                      return ps[:, 0:ncol]
                  wup = A.alloc([512], F32, parts=17)
                  P.dma("sp", wup, wup_in[L])
                  SP_ = A.alloc([8, 512], F32)
                  KTM = A.alloc([8, 512], BF16)
                  V = A.alloc([8, 1024], BF16)
                  LRT = A.alloc([TB], F32, parts=17)
                  tmpA = A.alloc([512], F32)
                  tmpB = A.alloc([512], F32)
                  P.memset(LRT, 1.0)
                  w = wblk("b0")
                  for half in range(2):
                      pv = proj_fm(w, 0, 16, half)
                      P.copy(LRT[0:16, half * 512:(half + 1) * 512], pv, eng="dve")
                  pv = proj_tm(w, 16, 128, 7)
                  P.copy(vh, pv, eng="dve")
                  for i in range(2):
                      w = wblk("gk", i)
                      for tt in range(8):
                          pv = proj_tm(w, 0, 256, tt)
                          P.copy(KTM[:, tt, i * 256:(i + 1) * 256], pv, eng=EV())
                  for i in range(4):
                      w = wblk("gv", i)
                      for tt in range(8):
                          pv = proj_tm(w, 0, 256, tt)
                          P.copy(V[:, tt, i * 256:(i + 1) * 256], pv, eng=EV())
                  for tt in range(8):
                      ps = PS()
                      P.mm(ps[:, 0:512], [(LRT[0:17, tt * 128:(tt + 1) * 128], wup[0:17, :])])
                      P.act(tmpA, ps[:, 0:512], AF.Exp, scale=-1.0)
                      P.act(SP_[:, tt, :], tmpA, AF.Ln, bias=1.0)
                      ps = PS()
                      P.mm(ps[:, 0:512], [(suff, SP_[:, tt, :])])
                      P.act(tmpB, ps[:, 0:512], AF.Exp, scale=-1.0 / 16)
                      P.tt(KTM[:, tt, :], KTM[:, tt, :], tmpB, ALU.mult)
                  dec = A.alloc([1], F32)
                  for tt in range(8):
                      for h in range(4):
                          psA = PS()
                          P.mm(psA[:, 0:128], [(SP_[:, tt, h * 128:(h + 1) * 128], tri)])
                          P.act(dec, psA[:, 127:128], AF.Exp, scale=-1.0 / 16)
                          psD = PS()
                          P.mm(psD[:, 0:256], [(KTM[:, tt, h * 128:(h + 1) * 128], V[:, tt, h * 256:(h + 1) * 256])])
                          P.stt(S[:, h, :], S[:, h, :], dec, psD[:, 0:256], ALU.mult, ALU.add)
                  for h in range(4):
                      P.copy(Sbf[:, h, :], S[:, h, :], eng="dve")
                  NH = 512
                  cosT = A.alloc([NH], F32)
                  sinT = A.alloc([NH], F32)
                  posi = A.alloc([NH], I32)
                  ang = A.alloc([NH], F32)
                  angi = A.alloc([NH], I32)
                  m1 = A.alloc([NH], F32)
                  qf = A.alloc([512], F32)
                  t1 = A.alloc([512], F32)
                  kr = A.alloc([512], BF16)
                  P.dma("sp", posi, pos_in[:, t0 + 512:t0 + TB].broadcast_to([128, NH]))
                  P.copy(ang, posi, eng="dve")
                  P.ts(ang, ang, invf, None, ALU.mult)
                  for which, dst in ((0, sinT), (1, cosT)):
                      if which == 1:
                          P.ts(ang, ang, 0.25, None, ALU.add)
                      P.copy(angi, ang, eng="dve")
                      P.copy(m1, angi, eng="dve")
                      P.tt(dst, ang, m1, ALU.subtract)
                      P.ts(m1, dst, 0.5, None, ALU.is_gt)
                      P.tt(dst, dst, m1, ALU.subtract)
                      P.ts(m1, dst, -0.5, None, ALU.is_lt)
                      P.tt(dst, dst, m1, ALU.add)
                      P.act(dst, dst, AF.Sin, scale=TWO_PI)
                  w = wblk("sk")
                  for kv in range(2):
                      pv = proj_fm(w, kv * 128, 128, 1)
                      P.copy(qf, pv, eng="act")
                      ps2 = PS()
                      P.mm(ps2[:, 0:512], [(Rm, qf)])
                      P.tt(t1, ps2[:, 0:512], sinT, ALU.mult)
                      P.tt(qf, qf, cosT, ALU.mult)
                      P.tt(kr, qf, t1, ALU.add)
                      P.copy(kTh[:, kv, :], kr[:, 384:512], eng="dve")
                  for g in range(4):
                      wu = wblk("pu", g)
                      for j in range(2):
                          pv = proj_fm(wu, j * 128, 128, 1)
                          P.copy(puh[:, g * 2 + j, :], pv[:, 496:512], eng=EV())
                  A.release()

              def full_pass(blk, dst_x, drow0):
                  t0 = blk * TB
                  first = (blk == 0)
                  A.mark()
                  xT = A.alloc([32, TB], BF16)
                  A.mark()
                  xt = [A.alloc([D], F32) for _ in range(2)]
                  for tt in range(8):
                      xtile = xt[tt % 2]
                      P.dma("sp", xtile, src_x[t0 + tt * 128:t0 + (tt + 1) * 128, :])
                      for g in range(8):
                          ps = PS()
                          for q in range(4):
                              kc = g * 4 + q
                              P.mm(ps[:, q * 128:(q + 1) * 128], [(xtile[:, kc * 128:(kc + 1) * 128], ident)])
                          P.copy(xT[:, g * 4:(g + 1) * 4, tt * 128:(tt + 1) * 128],
                                 ps[:, 0:512].rearrange("p (a b) -> p a b", a=4), eng=EV())
                  A.release()

                  A.mark()
                  wup = A.alloc([512], F32, parts=17)
                  P.dma("sp", wup, wup_in[L])
                  pscale = A.alloc([8], F32)
                  P.dma("sp", pscale, pscale_in[L])
                  gnb = A.alloc([1024], F32)
                  P.dma("sp", gnb, gnorm_in[L].broadcast_to([128, 1024]))
                  sinke = A.alloc([8], F32)
                  P.dma("sp", sinke, sink_in[L])
                  P.act(sinke, sinke, AF.Exp)

                  SC(1)
                  def wblk(name, i=0):
                      return w3(wload(win_in[L, WB[name] + i]), 32, 256)

                  def proj_fm(w, c0, ncol, half):
                      ps = PS()
                      P.mm(ps[0:ncol, 0:512], [(w[:, kc, c0:c0 + ncol], xT[:, kc, half * 512:(half + 1) * 512]) for kc in range(32)])
                      return ps[0:ncol, 0:512]

                  def proj_tm(w, c0, ncol, tt):
                      ps = PS()
                      P.mm(ps[:, 0:ncol], [(xT[:, kc, tt * 128:(tt + 1) * 128], w[:, kc, c0:c0 + ncol]) for kc in range(32)])
                      return ps[:, 0:ncol]

                  VS = A.alloc([9, 128], BF16)
                  stg = [A.alloc([512], BF16) for _ in range(4)]
                  sti = [0]

                  def ostage():
                      sti[0] = (sti[0] + 1) % 4
                      return stg[sti[0]]

                  def oflush(t, b, c, half):
                      P.dma("sp", oT_d[b, c, :, half * 512:(half + 1) * 512], t)
                  A.mark()
                  SP_ = A.alloc([8, 512], F32)
                  QT = A.alloc([4, TB], BF16)
                  KT = A.alloc([4, TB], BF16)
                  KTM = A.alloc([8, 512], BF16)
                  V = A.alloc([8, 1024], BF16)
                  GG = A.alloc([8, 1024], BF16)
                  LRT = A.alloc([TB], F32, parts=17)
                  tmpA = A.alloc([512], F32)
                  tmpB = A.alloc([512], F32)
                  P.memset(LRT, 1.0)
                  w = wblk("b0")
                  for half in range(2):
                      pv = proj_fm(w, 0, 16, half)
                      P.copy(LRT[0:16, half * 512:(half + 1) * 512], pv, eng="dve")
                  P.copy(VS[:, 0, :], vh, eng="dve")
                  for tt in range(8):
                      pv = proj_tm(w, 16, 128, tt)
                      P.copy(VS[:, tt + 1, :], pv, eng=EV())
                  P.copy(vh, VS[:, 8, :], eng="dve")
                  for i in range(2):
                      w = wblk("gk", i)
                      for j in range(2):
                          for half in range(2):
                              pv = proj_fm(w, j * 128, 128, half)
                              P.copy(KT[:, i * 2 + j, half * 512:(half + 1) * 512], pv, eng=EV())
                      for tt in range(8):
                          pv = proj_tm(w, 0, 256, tt)
                          P.copy(KTM[:, tt, i * 256:(i + 1) * 256], pv, eng=EV())
                  for i in range(2):
                      w = wblk("gq", i)
                      for j in range(2):
                          for half in range(2):
                              pv = proj_fm(w, j * 128, 128, half)
                              P.copy(QT[:, i * 2 + j, half * 512:(half + 1) * 512], pv, eng=EV())
                  for i in range(4):
                      w = wblk("gv", i)
                      for tt in range(8):
                          pv = proj_tm(w, 0, 256, tt)
                          P.copy(V[:, tt, i * 256:(i + 1) * 256], pv, eng=EV())
                  for i in range(4):
                      w = wblk("gg", i)
                      for tt in range(8):
                          pv = proj_tm(w, 0, 256, tt)
                          P.act(GG[:, tt, i * 256:(i + 1) * 256], pv, AF.Silu)
                  SC(2)
                  for tt in range(8):
                      ps = PS()
                      P.mm(ps[:, 0:512], [(LRT[0:17, tt * 128:(tt + 1) * 128], wup[0:17, :])])
                      P.act(tmpA, ps[:, 0:512], AF.Exp, scale=-1.0)
                      P.act(SP_[:, tt, :], tmpA, AF.Ln, bias=1.0)
                      ps = PS()
                      P.mm(ps[:, 0:512], [(suff, SP_[:, tt, :])])
                      P.act(tmpB, ps[:, 0:512], AF.Exp, scale=-1.0 / 16)
                      P.tt(KTM[:, tt, :], KTM[:, tt, :], tmpB, ALU.mult)
                  SC(3)
                  eb = A.alloc([128], F32)
                  enb = A.alloc([128], F32)
                  dec = A.alloc([1], F32)
                  ssq = A.alloc([1], F32)
                  rs = A.alloc([1], F32)
                  QE = A.alloc([128], BF16)
                  KN = A.alloc([128], BF16)
                  AT = A.alloc([128], BF16)
                  tmpO = A.alloc([256], F32)
                  junk = A.alloc([256], F32)
                  for tt in range(8):
                      tsl = slice(tt * 128, (tt + 1) * 128)
                      for h in range(4):
                          psA = PS()
                          P.mm(psA[:, 0:128], [(SP_[:, tt, h * 128:(h + 1) * 128], tri)])
                          P.act(eb, psA[:, 0:128], AF.Exp, scale=-1.0 / 16)
                          P.act(enb, psA[:, 0:128], AF.Exp, scale=1.0 / 16)
                          P.act(dec, psA[:, 127:128], AF.Exp, scale=-1.0 / 16)
                          P.stt(QE, QT[:, h, tsl], 128 ** -0.5, eb, ALU.mult, ALU.mult)
                          P.tt(KN, KT[:, h, tsl], enb, ALU.mult)
                          psB = PS()
                          P.mm(psB[:, 0:128], [(KN, QE)])
                          P.tt(AT, psB[:, 0:128], tri, ALU.mult)
                          psC = PS()
                          P.mm(psC[:, 0:256], [(AT, V[:, tt, h * 256:(h + 1) * 256]), (QE, Sbf[:, h, :])])
                          P.copy(tmpO, psC[:, 0:256], eng="act")
                          P.memset(ssq, 0.0)
                          P.act(junk, tmpO, AF.Square, accum_out=ssq)
                          P.act(rs, ssq, AF.Sqrt, bias=LN_EPS, scale=1.0 / 256)
                          P.recip(rs, rs)
                          P.stt(tmpO, tmpO, rs, gnb[:, h * 256:(h + 1) * 256], ALU.mult, ALU.mult)
                          P.tt(GG[:, tt, h * 256:(h + 1) * 256], GG[:, tt, h * 256:(h + 1) * 256], tmpO, ALU.mult)
                          psD = PS()
                          P.mm(psD[:, 0:256], [(KTM[:, tt, h * 128:(h + 1) * 128], V[:, tt, h * 256:(h + 1) * 256])])
                          P.stt(S[:, h, :], S[:, h, :], dec, psD[:, 0:256], ALU.mult, ALU.add)
                          P.copy(Sbf[:, h, :], S[:, h, :], eng="dve")
                      for g in range(2):
                          ps = PS()
                          for q in range(4):
                              c = g * 4 + q
                              P.mm(ps[:, q * 128:(q + 1) * 128], [(GG[:, tt, c * 128:(c + 1) * 128], identb)])
                          st_ = ostage()
                          P.copy(st_, ps[:, 0:512], eng=EV())
                          P.dma("sp", oT_d[1, g * 4:(g + 1) * 4, :, tsl].rearrange("c p t -> p c t"), st_.rearrange("p (a b) -> p a b", a=4))

                  A.release()
                  SC(4)
                  A.mark()
                  QR = A.alloc([8, TB], BF16)
                  KTs = A.alloc([2, 128 + TB], BF16)
                  cosT = A.alloc([TB], F32)
                  sinT = A.alloc([TB], F32)
                  posi = A.alloc([TB], I32)
                  ang = A.alloc([TB], F32)
                  angi = A.alloc([TB], I32)
                  m1 = A.alloc([TB], F32)
                  qf = A.alloc([512], F32)
                  t1 = A.alloc([512], F32)
                  P.dma("sp", posi, pos_in[:, t0:t0 + TB].broadcast_to([128, TB]))
                  P.copy(ang, posi, eng="dve")
                  P.ts(ang, ang, invf, None, ALU.mult)
                  for which, dst in ((0, sinT), (1, cosT)):
                      if which == 1:
                          P.ts(ang, ang, 0.25, None, ALU.add)
                      P.copy(angi, ang, eng="dve")
                      P.copy(m1, angi, eng="dve")
                      P.tt(dst, ang, m1, ALU.subtract)
                      P.ts(m1, dst, 0.5, None, ALU.is_gt)
                      P.tt(dst, dst, m1, ALU.subtract)
                      P.ts(m1, dst, -0.5, None, ALU.is_lt)
                      P.tt(dst, dst, m1, ALU.add)
                      P.act(dst, dst, AF.Sin, scale=TWO_PI)

                  def rope(pv, half, dst):
                      hs = slice(half * 512, (half + 1) * 512)
                      P.copy(qf, pv, eng="act")
                      ps2 = PS()
                      P.mm(ps2[:, 0:512], [(Rm, qf)])
                      P.tt(t1, ps2[:, 0:512], sinT[:, hs], ALU.mult)
                      P.tt(qf, qf, cosT[:, hs], ALU.mult)
                      P.tt(dst, qf, t1, ALU.add)
                  for i in range(4):
                      w = wblk("sq", i)
                      for j in range(2):
                          for half in range(2):
                              pv = proj_fm(w, j * 128, 128, half)
                              rope(pv, half, QR[:, i * 2 + j, half * 512:(half + 1) * 512])
                  w = wblk("sk")
                  for kv in range(2):
                      P.copy(KTs[:, kv, 0:128], kTh[:, kv, :], eng="dve")
                      for half in range(2):
                          pv = proj_fm(w, kv * 128, 128, half)
                          rope(pv, half, KTs[:, kv, 128 + half * 512:128 + (half + 1) * 512])
                      P.copy(kTh[:, kv, :], KTs[:, kv, TB:TB + 128], eng="dve")
                  SC(5)
                  Pm = A.alloc([9, 512], BF16)
                  et = A.alloc([512], F32)
                  rden = A.alloc([TB], F32)
                  gsw = A.alloc([TB], BF16)
                  for c in range(8):
                      kv = c // 4
                      wg = wblk("sg", c // 2) if c % 2 == 0 else wg
                      for half in range(2):
                          pv = proj_fm(wg, (c % 2) * 128, 128, half)
                          P.act(gsw[:, half * 512:(half + 1) * 512], pv, AF.Silu)
                      jt0 = 1 if first else 0
                      for jt in range(jt0, 9):
                          j = jt - 1
                          q_lo = max(j, 0) * 128
                          q_hi = min(j + 2, 8) * 128
                          n = q_hi - q_lo
                          off = 0 if j >= 0 else 128
                          for hh in range(2):
                              hp = slice(hh * 64, (hh + 1) * 64)
                              ps = PS()
                              P.mm(ps[:, off:off + n],
                                   [(KTs[hp, kv, jt * 128:(jt + 1) * 128], QR[hp, c, q_lo:q_hi])])
                              P.act(et[:, hh * 256 + off:hh * 256 + off + n], ps[:, off:off + n], AF.Exp, scale=0.125)
                              if j >= 0:
                                  P.tt(Pm[:, jt, hh * 256:hh * 256 + 128], et[:, hh * 256:hh * 256 + 128], msame, ALU.mult)
                              if j < 7:
                                  P.tt(Pm[:, jt, hh * 256 + 128:hh * 256 + 256], et[:, hh * 256 + 128:hh * 256 + 256], mnext, ALU.mult)
                      SC(51)
                      for half in range(2):
                          pso = PS()
                          pss_ = PS()
                          for qi in range(4):
                              qt = half * 4 + qi
                              contrib = []
                              if not (first and qt == 0):
                                  contrib.append((qt, 128))
                              contrib.append((qt + 1, 0))
                              for hh in range(2):
                                  hp = slice(hh * 64, (hh + 1) * 64)
                                  P.mm(pso[hp, qi * 128:(qi + 1) * 128],
                                       [(VS[:, jt, kv * 64:(kv + 1) * 64], Pm[:, jt, hh * 256 + o_:hh * 256 + o_ + 128]) for jt, o_ in contrib])
                                  P.mm(pss_[hp, qi * 128:(qi + 1) * 128],
                                       [(onesb[:, 0:64], Pm[:, jt, hh * 256 + o_:hh * 256 + o_ + 128]) for jt, o_ in contrib])
                          SC(52)
                          hs = slice(half * 512, (half + 1) * 512)
                          P.ts(rden[:, hs], pss_[:, 0:512], sinke[:, c:c + 1], None, ALU.add)
                          P.recip(rden[:, hs], rden[:, hs])
                          P.tt(rden[:, hs], rden[:, hs], pso[:, 0:512], ALU.mult)
                          st_ = ostage()
                          P.tt(st_, rden[:, hs], gsw[:, hs], ALU.mult)
                          oflush(st_, 2, c, half)
                  A.release()
                  SC(6)
                  A.mark()
                  mkT = A.alloc([8, 256], BF16)
                  mv = A.alloc([2, 1024], BF16)
                  A.mark()
                  memT = A.alloc([32, 256], BF16)
                  mtile = A.alloc([D], F32)
                  for mt in range(2):
                      P.dma("sp", mtile, mem_in[mt * 128:(mt + 1) * 128, :])
                      for g in range(8):
                          ps = PS()
                          for q in range(4):
                              kc = g * 4 + q
                              P.mm(ps[:, q * 128:(q + 1) * 128], [(mtile[:, kc * 128:(kc + 1) * 128], ident)])
                          P.copy(memT[:, g * 4:(g + 1) * 4, mt * 128:(mt + 1) * 128],
                                 ps[:, 0:512].rearrange("p (a b) -> p a b", a=4), eng=EV())
                  for b in range(8):
                      w = w3(wload(wmk_in[L, b]), 32, 256)
                      if b < 4:
                          for j in range(2):
                              ps = PS()
                              P.mm(ps[:, 0:256], [(w[:, kc, j * 128:(j + 1) * 128], memT[:, kc, :]) for kc in range(32)])
                              P.copy(mkT[:, b * 2 + j, :], ps[:, 0:256], eng=EV())
                      else:
                          for mt in range(2):
                              ps = PS()
                              P.mm(ps[:, 0:256], [(memT[:, kc, mt * 128:(mt + 1) * 128], w[:, kc, :]) for kc in range(32)])
                              P.copy(mv[:, mt, (b - 4) * 256:(b - 3) * 256], ps[:, 0:256], eng=EV())
                  A.release()

                  Qm = A.alloc([2, TB], BF16)
                  E = A.alloc([2, 512], BF16)
                  rsum = A.alloc([512], F32)
                  gm = A.alloc([512], F32)
                  for h in range(4):
                      wq = wblk("xq", h)
                      for dc in range(2):
                          for half in range(2):
                              pv = proj_fm(wq, dc * 128, 128, half)
                              P.copy(Qm[:, dc, half * 512:(half + 1) * 512], pv, eng=EV())
                      wgm = wblk("xg", h)
                      for half in range(2):
                          hs = slice(half * 512, (half + 1) * 512)
                          for mt in range(2):
                              ps = PS()
                              P.mm(ps[:, 0:512], [(mkT[:, h * 2 + dc, mt * 128:(mt + 1) * 128], Qm[:, dc, hs]) for dc in range(2)])
                              P.act(E[:, mt, :], ps[:, 0:512], AF.Exp, scale=1.0 / 16)
                          ps = PS()
                          P.mm(ps[:, 0:512], [(onesb, E[:, mt, :]) for mt in range(2)])
                          P.recip(rsum, ps[:, 0:512])
                          for dc in range(2):
                              pg_ = proj_fm(wgm, dc * 128, 128, half)
                              P.act(gm, pg_, AF.Silu)
                              P.tt(gm, gm, rsum, ALU.mult)
                              ps = PS()
                              P.mm(ps[:, 0:512], [(mv[:, mt, h * 256 + dc * 128:h * 256 + (dc + 1) * 128], E[:, mt, :]) for mt in range(2)])
                              st_ = ostage()
                              P.tt(st_, ps[:, 0:512], gm, ALU.mult)
                              oflush(st_, 3, h * 2 + dc, half)
                  A.release()
                  SC(7)
                  A.mark()
                  wpool = A.alloc([4, 2, 256], BF16)
                  P.dma("pool", wpool, wpool_in[L].rearrange("p (g j d) -> p g j d", g=4, j=2))
                  U = A.alloc([2, 16 + TB], F32)
                  Sa = A.alloc([16 + TB], F32)
                  Sb = A.alloc([16 + TB], F32)
                  Pb = A.alloc([2, TB], BF16)
                  gp = A.alloc([512], F32)
                  for g in range(4):
                      wdw = 2 ** (g + 1)
                      wu = wblk("pu", g)
                      for j in range(2):
                          P.copy(U[:, j, 0:16], puh[:, g * 2 + j, :], eng="dve")
                          for half in range(2):
                              pv = proj_fm(wu, j * 128, 128, half)
                              P.copy(U[:, j, 16 + half * 512:16 + (half + 1) * 512], pv, eng=EV())
                          P.copy(puh[:, g * 2 + j, :], U[:, j, TB:TB + 16], eng="dve")
                          cur = U[:, j, :]
                          bufs = [Sa, Sb]
                          lo = 0
                          for k in range(g + 1):
                              sh = 2 ** k
                              nxt = bufs[k % 2]
                              lo2 = lo + sh
                              P.tt(nxt[:, lo2:16 + TB], cur[:, lo2:16 + TB], cur[:, lo2 - sh:16 + TB - sh], ALU.add)
                              cur = nxt
                              lo = lo2
                          P.stt(Pb[:, j, :], cur[:, 16:16 + TB], 1.0 / wdw, U[:, j, 16:16 + TB], ALU.mult, ALU.subtract)
                          if first:
                              P.tt(Sa[:, 0:16] if cur is not Sa else Sb[:, 0:16], cur[:, 16:32], invc16[:, g * 16:(g + 1) * 16], ALU.mult)
                              tmp16 = Sa[:, 0:16] if cur is not Sa else Sb[:, 0:16]
                              P.tt(Pb[:, j, 0:16], tmp16, U[:, j, 16:32], ALU.subtract)
                      wgp = wblk("pg", g)
                      for dj in range(2):
                          for half in range(2):
                              hs = slice(half * 512, (half + 1) * 512)
                              pg_ = proj_fm(wgp, dj * 128, 128, half)
                              P.act(gp, pg_, AF.Silu)
                              ps = PS()
                              P.mm(ps[:, 0:512], [(wpool[:, g, cj, dj * 128:(dj + 1) * 128], Pb[:, cj, hs]) for cj in range(2)])
                              st_ = ostage()
                              P.stt(st_, ps[:, 0:512], pscale[:, g * 2 + dj:g * 2 + dj + 1], gp, ALU.mult, ALU.mult)
                              oflush(st_, 0, g * 2 + dj, half)
                  A.release()
                  SC(8)
                  A.release()
                  oT = A.alloc([4, 8, TB], BF16)
                  for b in range(4):
                      P.dma("sp", oT[:, b, :, :], oT_d[b].rearrange("c p t -> p c t"))
                  if debug:
                      A.mark()
                      dtile = A.alloc([TB], F32)
                      for b in range(4):
                          for c in range(8):
                              P.copy(dtile, oT[:, b, c, :], eng="dve")
                              P.dma("sp", dbg_out[b, c], dtile)
                      A.release()
                  A.mark()
                  wbs = [A.alloc([2, 8, 128], BF16) for _ in range(2)]
                  yacc = A.alloc([TB], F32)
                  sg_ = A.alloc([512], F32)
                  ybf = A.alloc([TB], BF16)
                  for dc in range(32):
                      for pr in range(2):
                          w = w3(wload(win_in[L, WB["mg"] + dc * 2 + pr]), 32, 256)
                          wb_ = wbs[pr]
                          P.dma("pool", wb_, wbr_in[L, dc * 2 + pr].rearrange("p (b k c) -> p b k c", b=2, k=8))
                          for bi in range(2):
                              b = pr * 2 + bi
                              for half in range(2):
                                  hs = slice(half * 512, (half + 1) * 512)
                                  pv = proj_fm(w, bi * 128, 128, half)
                                  P.act(sg_, pv, AF.Sigmoid)
                                  ps = PS()
                                  P.mm(ps[:, 0:512], [(wb_[:, bi, kc, :], oT[:, b, kc, hs]) for kc in range(8)])
                                  if b == 0:
                                      P.tt(yacc[:, hs], ps[:, 0:512], sg_, ALU.mult)
                                  else:
                                      P.tt(sg_, ps[:, 0:512], sg_, ALU.mult)
                                      P.tt(yacc[:, hs], yacc[:, hs], sg_, ALU.add)
                      P.copy(ybf, yacc, eng="act")
                      P.dma("sp", yT_d[dc], ybf)
                  A.release()
                  A.release()
                  SC(9)
                  A.mark()
                  sums = A.alloc([8, 16], F32)
                  sqs = A.alloc([8, 16], F32)
                  P.memset(sums, 0.0)
                  P.memset(sqs, 0.0)
                  A.mark()
                  yT = A.alloc([32, TB], BF16)
                  P.dma("sp", yT, yT_d.rearrange("c p t -> p c t"))
                  xb = [A.alloc([256], F32) for _ in range(2)]
                  vb = [A.alloc([256], F32) for _ in range(2)]
                  jk = A.alloc([256], F32)
                  k_ = 0
                  for cbk in range(16):
                      w = w3(wload(wout_in[L, cbk]), 32, 256)
                      for tt in range(8):
                          xx = xb[k_ % 2]
                          vv = vb[k_ % 2]
                          k_ += 1
                          P.dma("sp", xx, src_x[t0 + tt * 128:t0 + (tt + 1) * 128, cbk * 256:(cbk + 1) * 256])
                          ps = PS()
                          P.mm(ps[:, 0:256], [(yT[:, kc, tt * 128:(tt + 1) * 128], w[:, kc, :]) for kc in range(32)])
                          P.stt(vv, xx, ALPHA, ps[:, 0:256], ALU.mult, ALU.add)
                          P.act(jk, vv, AF.Identity, accum_out=sums[:, tt, cbk:cbk + 1])
                          P.act(jk, vv, AF.Square, accum_out=sqs[:, tt, cbk:cbk + 1])
                          P.dma("sp", v_d[tt * 128:(tt + 1) * 128, cbk * 256:(cbk + 1) * 256], vv)
                  A.release()
                  gb = A.alloc([D], F32)
                  bb = A.alloc([D], F32)
                  P.dma("sp", gb, lng_in[L].broadcast_to([128, D]))
                  P.dma("sp", bb, lnb_in[L].broadcast_to([128, D]))
                  vt = [A.alloc([D], F32) for _ in range(2)]
                  mean = A.alloc([8], F32)
                  msq = A.alloc([8], F32)
                  rstd = A.alloc([8], F32)
                  jk16 = A.alloc([16], F32)

                  def red(dst, src):
                      def fn(e):
                          return e.reduce_sum(dst, src, AX.X)
                      return P.add("dve", fn, reads=[src], writes=[dst])
                  for tt in range(8):
                      c1 = slice(tt, tt + 1)
                      P.memset(mean[:, c1], 0.0)
                      P.memset(msq[:, c1], 0.0)
                      P.act(jk16, sums[:, tt, :], AF.Identity, accum_out=mean[:, c1])
                      P.act(jk16, sqs[:, tt, :], AF.Identity, accum_out=msq[:, c1])
                      P.ts(mean[:, c1], mean[:, c1], 1.0 / D, None, ALU.mult)
                      P.ts(msq[:, c1], msq[:, c1], 1.0 / D, None, ALU.mult)
                      P.tt(rstd[:, c1], mean[:, c1], mean[:, c1], ALU.mult)
                      P.tt(rstd[:, c1], msq[:, c1], rstd[:, c1], ALU.subtract)
                      P.act(rstd[:, c1], rstd[:, c1], AF.Sqrt, bias=LN_EPS)
                      P.recip(rstd[:, c1], rstd[:, c1])
                      v_ = vt[tt % 2]
                      P.dma("sp", v_, v_d[tt * 128:(tt + 1) * 128, :])
                      P.ts(v_, v_, mean[:, c1], rstd[:, c1], ALU.subtract, ALU.mult)
                      P.tt(v_, v_, gb, ALU.mult)
                      P.tt(v_, v_, bb, ALU.add)
                      P.dma("sp", dst_x[drow0 + tt * 128:drow0 + (tt + 1) * 128, :], v_)
                  A.release()
              last = (layer == n_layers - 1)
              for blk in range(n_blocks):
                  if not ranked:
                      full_pass(blk, dst_x, blk * TB)
                  elif not last:
                      if blk > 0:
                          P.region_begin("ge", blk)
                      full_pass(blk, dst_x, blk * TB)
                      if blk > 0:
                          P.region_end()
                  else:
                      if blk < n_blocks - 1:
                          P.region_begin("ge", blk + 1)
                          state_pass(blk)
                          P.region_end()
                      P.region_begin("eq", blk)
                      full_pass(blk, dst_x, 0)
                      P.region_end()
        except _Stop:
            pass
        P.emit()
    return nc


def _prep(inputs):
    cols, WB = _win_cols()
    f = np.float32
    w_in = np.asarray(inputs["w_in"], f)
    win = np.empty((2, NBLK_W, 128, 32 * 256), f)
    for l in range(2):
        wp = w_in[l][:, cols]
        win[l] = wp.reshape(32, 128, NBLK_W, 256).transpose(2, 1, 0, 3).reshape(NBLK_W, 128, 32 * 256)
    wmk = np.asarray(inputs["w_mem_kv"], f).reshape(2, 32, 128, 8, 256).transpose(0, 3, 2, 1, 4).reshape(2, 8, 128, 32 * 256)
    wout = np.asarray(inputs["w_out"], f).reshape(2, 32, 128, 16, 256).transpose(0, 3, 2, 1, 4).reshape(2, 16, 128, 32 * 256)
    wb = np.asarray(inputs["w_branch"], f).reshape(2, 2, 2, 8, 128, 32, 128)
    wbr = wb.transpose(0, 5, 1, 4, 2, 3, 6).reshape(2, 64, 128, 2 * 8 * 128)
    wpool = np.asarray(inputs["w_pool"], f).reshape(2, 4, 2, 128, 256).transpose(0, 3, 1, 2, 4).reshape(2, 128, 4 * 2 * 256)
    wup = np.concatenate([np.asarray(inputs["w_gla_up"], f), np.asarray(inputs["b_gla"], f)[:, None, :]], axis=1)
    pscale = np.asarray(inputs["pool_scale"], f).reshape(2, 8, 128).transpose(0, 2, 1)
    gnorm = np.asarray(inputs["gla_norm"], f).reshape(2, 1, 1024)
    sk = np.asarray(inputs["sinks"], f)
    sink = np.repeat(sk.reshape(2, 8, 2), 64, axis=2).transpose(0, 2, 1)
    lng = np.asarray(inputs["ln_g"], f).reshape(2, 1, D)
    lnb = np.asarray(inputs["ln_b"], f).reshape(2, 1, D)
    cst = np.zeros((128, 1024), f)
    j = np.arange(128)
    cst[:, 0:128] = np.eye(128)
    cst[:, 128:256] = (j[:, None] <= j[None, :])
    cst[:, 256:384] = (j[:, None] > j[None, :])
    R = np.zeros((128, 128), f)
    for p in range(128):
        jj = p % 64
        if jj < 8:
            R[p + 8, p] = -1.0
        elif jj < 16:
            R[p - 8, p] = 1.0
    cst[:, 384:512] = R
    cst[:, 512:640] = (j[None, :] >= j[:, None])
    cst[:, 640:768] = (j[None, :] < j[:, None])
    jj = j % 64
    inv = np.where(jj < 16, 500000.0 ** (-((jj % 8) * 2.0) / 16.0), 0.0)
    cst[:, 768] = (inv / TWO_PI)
    for g, wd in enumerate((2, 4, 8, 16)):
        cst[:, 832 + g * 16:832 + (g + 1) * 16] = 1.0 / np.minimum(np.arange(16) + 1, wd)
    shared = dict(win=win, wmk=np.ascontiguousarray(wmk), wbr=np.ascontiguousarray(wbr), wout=np.ascontiguousarray(wout),
                  wpool=np.ascontiguousarray(wpool), wup=np.ascontiguousarray(wup), pscale=np.ascontiguousarray(pscale),
                  gnorm=gnorm, sink=np.ascontiguousarray(sink), lng=lng, lnb=lnb, cst=cst)
    return shared


def kernel(**inputs):
    shared = _prep(inputs)
    x = np.asarray(inputs["x"], np.float32)
    mem = np.asarray(inputs["mem"], np.float32)
    pos = np.asarray(inputs["positions"], np.int32)
    nc = build()
    in_maps = []
    for c in range(8):
        b, r = c // 4, c % 4
        m = dict(shared)
        m["x"] = np.ascontiguousarray(x[b])
        m["mem"] = np.ascontiguousarray(mem[b])
        m["pos"] = np.ascontiguousarray(pos[b:b + 1])
        m["rk"] = np.array([[r]], np.int32)
        in_maps.append(m)
    res = run_bass_kernel_spmd(nc, in_maps, core_ids=list(range(8)))
    out = np.empty((2, SEQ, D), np.float32)
    for c in range(8):
        b, r = c // 4, c % 4
        out[b, r * TB:(r + 1) * TB] = res.results[c]["y"]
    return out
```
